# Optimizing a Trainium2 kernel written in Bass

```python
import math
import jax, jax.numpy as jnp
from jax import lax
import numpy as np

D_MODEL = 1024
BATCH = 16
SEQ = 2048
DEPTH = 4

CHUNK = 64
Q_BLOCK = 128
N_MEM = 256
N_A_LAYERS = DEPTH // 2
N_B_LAYERS = DEPTH - N_A_LAYERS
EPS = 1e-6
M_HEADS = 4
M_HEAD_DIM = D_MODEL // M_HEADS
M_WIDTH = M_HEADS * M_HEAD_DIM
CONV_W = 4
B_HEADS = 8
QK_NOPE = 128
QK_ROPE = 64
QK_HEAD = QK_NOPE + QK_ROPE
V_HEAD = 128
B_WIDTH = B_HEADS * V_HEAD
Q_LORA = 384
KV_LORA = 256
ROPE_THETA = 10000.0
MEM_HEADS = 4
MEM_HEAD_DIM = 128
MEM_WIDTH = MEM_HEADS * MEM_HEAD_DIM
A_IN = 2 * M_WIDTH + 3 * M_WIDTH + 2 * M_HEADS + 2 * MEM_WIDTH
B_IN = Q_LORA + B_WIDTH + 2 * MEM_WIDTH
A_MIX = M_WIDTH + MEM_WIDTH
B_MIX = B_WIDTH + MEM_WIDTH

kernel_name = "yoco_mlstm_mla_memory_trunk"


def rmsnorm(t, g):
    tf = t.astype(jnp.float32)
    y = tf * lax.rsqrt(jnp.mean(tf * tf, axis=-1, keepdims=True) + EPS)
    return (y * g.astype(jnp.float32)).astype(t.dtype)


def split_cols(t, sizes):
    offs = np.cumsum(np.array(sizes))[:-1].tolist()
    return jnp.split(t, offs, axis=-1)


def rope(t, positions):
    half = QK_ROPE // 2
    inv = ROPE_THETA ** (-jnp.arange(0, QK_ROPE, 2, dtype=jnp.float32) / QK_ROPE)
    ang = positions.astype(jnp.float32)[..., None] * inv
    cos = jnp.cos(ang)[:, :, None, :]
    sin = jnp.sin(ang)[:, :, None, :]
    tf = t.astype(jnp.float32)
    t1, t2 = tf[..., :half], tf[..., half:]
    return jnp.concatenate([t1 * cos - t2 * sin, t1 * sin + t2 * cos], axis=-1).astype(t.dtype)


def causal_conv(u, w, b):
    out = lax.conv_general_dilated(
        u, w[:, None, :].astype(u.dtype), window_strides=(1,), padding=[(CONV_W - 1, 0)],
        dimension_numbers=("NWC", "WIO", "NWC"), feature_group_count=u.shape[-1])
    return out + b.astype(u.dtype)


def mlstm_chunkwise(q, k, v, i_pre, f_pre):
    bsz, s, h, dh = q.shape
    nc = s // CHUNK

    def to_chunks(t):
        return t.astype(jnp.float32).reshape(bsz, nc, CHUNK, h, -1).transpose(1, 0, 3, 2, 4)

    def gate_chunks(t):
        return t.astype(jnp.float32).reshape(bsz, nc, CHUNK, h).transpose(1, 0, 3, 2)

    qc = to_chunks(q) * (dh ** -0.5)
    kc = to_chunks(k)
    vc = to_chunks(v)
    ic = gate_chunks(i_pre)
    lfc = jax.nn.log_sigmoid(gate_chunks(f_pre))
    causal = jnp.tril(jnp.ones((CHUNK, CHUNK), dtype=bool))

    def step(carry, inp):
        c_mat, n_vec, m = carry
        qb, kb, vb, ib, fb = inp
        b = jnp.cumsum(fb, axis=-1)
        d = b[..., :, None] - b[..., None, :] + ib[..., None, :]
        d = jnp.where(causal, d, -jnp.inf)
        inter = b + m[..., None]
        m_t = jnp.maximum(inter, jnp.max(d, axis=-1))
        p = jnp.einsum("bhtd,bhsd->bhts", qb, kb) * jnp.exp(d - m_t[..., None])
        g = jnp.exp(inter - m_t)
        num = jnp.einsum("bhts,bhsv->bhtv", p, vb) + g[..., None] * jnp.einsum("bhvk,bhtk->bhtv", c_mat, qb)
        den = jnp.sum(p, axis=-1) + g * jnp.einsum("bhk,bhtk->bht", n_vec, qb)
        h_out = num / jnp.maximum(jnp.abs(den), jnp.exp(-m_t))[..., None]
        b_last = b[..., -1]
        a = b_last[..., None] - b + ib
        m_new = jnp.maximum(b_last + m, jnp.max(a, axis=-1))
        wa = jnp.exp(a - m_new[..., None])
        gs = jnp.exp(b_last + m - m_new)
        c_new = gs[..., None, None] * c_mat + jnp.einsum("bhs,bhsv,bhsk->bhvk", wa, vb, kb)
        n_new = gs[..., None] * n_vec + jnp.einsum("bhs,bhsk->bhk", wa, kb)
        return (c_new, n_new, m_new), h_out

    init = (jnp.zeros((bsz, h, dh, dh), jnp.float32), jnp.zeros((bsz, h, dh), jnp.float32),
            jnp.zeros((bsz, h), jnp.float32))
    _, hs = lax.scan(step, init, (qc, kc, vc, ic, lfc))
    return hs.transpose(1, 0, 3, 2, 4).reshape(bsz, s, h, dh)


def chunk_causal_attention(q, k, v):
    s = q.shape[1]
    scale = QK_HEAD ** -0.5
    outs = []
    for j in range(s // Q_BLOCK):
        qs = j * Q_BLOCK
        ke = qs + Q_BLOCK
        sc = jnp.einsum("bqhd,bkhd->bhqk", q[:, qs:ke], k[:, :ke],
                        preferred_element_type=jnp.float32) * scale
        q_chunk = (qs + jnp.arange(Q_BLOCK)) // CHUNK
        k_chunk = jnp.arange(ke) // CHUNK
        sc = jnp.where(k_chunk[None, :] <= q_chunk[:, None], sc, -jnp.inf)
        p = jax.nn.softmax(sc, axis=-1).astype(v.dtype)
        outs.append(jnp.einsum("bhqk,bkhd->bqhd", p, v[:, :ke]))
    return jnp.concatenate(outs, axis=1)


def memory_kv(mem_n, w_kv, k_gain):
    bsz, nm, _ = mem_n.shape
    mk, mv = split_cols(mem_n @ w_kv, [MEM_WIDTH, MEM_WIDTH])
    mk = rmsnorm(mk.reshape(bsz, nm, MEM_HEADS, MEM_HEAD_DIM), k_gain)
    return mk, mv.reshape(bsz, nm, MEM_HEADS, MEM_HEAD_DIM)


def memory_attention(mq, mz, mk, mv, q_gain):
    bsz, s, _ = mq.shape
    q = rmsnorm(mq.reshape(bsz, s, MEM_HEADS, MEM_HEAD_DIM), q_gain)
    sc = jnp.einsum("bqhd,bkhd->bhqk", q, mk, preferred_element_type=jnp.float32) * (MEM_HEAD_DIM ** -0.5)
    p = jax.nn.softmax(sc, axis=-1).astype(mv.dtype)
    out = jnp.einsum("bhqk,bkhd->bqhd", p, mv).reshape(bsz, s, MEM_WIDTH)
    return out * jax.nn.silu(mz)


def mlstm_layer(x, norm_g, w_in, conv_w, conv_b, ig_b, fg_b, h_g, w_out, mk, mv, mq_gain):
    bsz, s, _ = x.shape
    h = rmsnorm(x, norm_g)
    qk, v, o, z, i_pre, f_pre, mq, mz = split_cols(
        h @ w_in, [2 * M_WIDTH, M_WIDTH, M_WIDTH, M_WIDTH, M_HEADS, M_HEADS, MEM_WIDTH, MEM_WIDTH])
    qk = jax.nn.silu(causal_conv(qk, conv_w, conv_b))
    q, k = split_cols(qk, [M_WIDTH, M_WIDTH])
    heads = (bsz, s, M_HEADS, M_HEAD_DIM)
    ht = mlstm_chunkwise(q.reshape(heads), k.reshape(heads), v.reshape(heads),
                         i_pre + ig_b, f_pre + fg_b).astype(x.dtype)
    ht = rmsnorm(ht, h_g.reshape(M_HEADS, M_HEAD_DIM)).reshape(bsz, s, M_WIDTH)
    y_m = jax.nn.sigmoid(o) * ht * jax.nn.silu(z)
    y_mem = memory_attention(mq, mz, mk, mv, mq_gain)
    return x + jnp.concatenate([y_m, y_mem], axis=-1) @ w_out


def shared_kv(x, positions, kv_norm, w_kv_a, kv_lat_norm, w_kv_b, k_gain):
    bsz, s, _ = x.shape
    h = rmsnorm(x, kv_norm)
    c_kv, k_pe = split_cols(h @ w_kv_a, [KV_LORA, QK_ROPE])
    kv = (rmsnorm(c_kv, kv_lat_norm) @ w_kv_b).reshape(bsz, s, B_HEADS, QK_NOPE + V_HEAD)
    k_nope, v = kv[..., :QK_NOPE], kv[..., QK_NOPE:]
    k = jnp.concatenate([k_nope, jnp.broadcast_to(k_pe[:, :, None, :], (bsz, s, B_HEADS, QK_ROPE))], axis=-1)
    k = rmsnorm(k, k_gain)
    k = jnp.concatenate([k[..., :QK_NOPE], rope(k[..., QK_NOPE:], positions)], axis=-1)
    return k, v


def mla_layer(x, positions, k_sh, v_sh, norm_g, w_in, q_lat_g, w_q_up, q_gain, w_out, mk, mv, mq_gain):
    bsz, s, _ = x.shape
    h = rmsnorm(x, norm_g)
    q_lat, z, mq, mz = split_cols(h @ w_in, [Q_LORA, B_WIDTH, MEM_WIDTH, MEM_WIDTH])
    q = (rmsnorm(q_lat, q_lat_g) @ w_q_up).reshape(bsz, s, B_HEADS, QK_HEAD)
    q = rmsnorm(q, q_gain)
    q = jnp.concatenate([q[..., :QK_NOPE], rope(q[..., QK_NOPE:], positions)], axis=-1)
    attn = chunk_causal_attention(q, k_sh, v_sh).reshape(bsz, s, B_WIDTH)
    y_b = attn * jax.nn.silu(z)
    y_mem = memory_attention(mq, mz, mk, mv, mq_gain)
    return x + jnp.concatenate([y_b, y_mem], axis=-1) @ w_out


def setup_inputs(seed: int = 0) -> dict:
    key = jax.random.key(seed)
    ks = jax.random.split(key, 32)

    def nrm(k, shape, scale):
        return jax.random.normal(k, shape, jnp.float32) * scale

    def gain(k, shape):
        return 1.0 + 0.02 * jax.random.normal(k, shape, jnp.float32)

    offsets = jax.random.randint(ks[2], (BATCH,), 0, 4096, dtype=jnp.int32)
    positions = (offsets[:, None] + jnp.arange(SEQ, dtype=jnp.int32)[None, :]).astype(jnp.int32)
    fg_base = jnp.linspace(3.0, 6.0, M_HEADS, dtype=jnp.float32)
    return {
        "x": nrm(ks[0], (BATCH, SEQ, D_MODEL), 1.0),
        "mem": nrm(ks[1], (BATCH, N_MEM, D_MODEL), 1.0),
        "positions": positions,
        "a_norm": gain(ks[3], (N_A_LAYERS, D_MODEL)),
        "a_w_in": nrm(ks[4], (N_A_LAYERS, D_MODEL, A_IN), D_MODEL ** -0.5),
        "a_conv_w": nrm(ks[5], (N_A_LAYERS, CONV_W, 2 * M_WIDTH), CONV_W ** -0.5),
        "a_conv_b": nrm(ks[6], (N_A_LAYERS, 2 * M_WIDTH), 0.01),
        "a_ig_bias": nrm(ks[7], (N_A_LAYERS, M_HEADS), 0.1),
        "a_fg_bias": fg_base[None, :] + nrm(ks[8], (N_A_LAYERS, M_HEADS), 0.1),
        "a_h_norm": gain(ks[9], (N_A_LAYERS, M_WIDTH)),
        "a_w_out": nrm(ks[10], (N_A_LAYERS, A_MIX, D_MODEL), A_MIX ** -0.5),
        "b_norm": gain(ks[11], (N_B_LAYERS, D_MODEL)),
        "b_w_in": nrm(ks[12], (N_B_LAYERS, D_MODEL, B_IN), D_MODEL ** -0.5),
        "b_q_lat_norm": gain(ks[13], (N_B_LAYERS, Q_LORA)),
        "b_w_q_up": nrm(ks[14], (N_B_LAYERS, Q_LORA, B_HEADS * QK_HEAD), Q_LORA ** -0.5),
        "b_q_gain": gain(ks[15], (N_B_LAYERS, QK_HEAD)),
        "b_w_out": nrm(ks[16], (N_B_LAYERS, B_MIX, D_MODEL), B_MIX ** -0.5),
        "kv_norm": gain(ks[17], (D_MODEL,)),
        "w_kv_a": nrm(ks[18], (D_MODEL, KV_LORA + QK_ROPE), D_MODEL ** -0.5),
        "kv_lat_norm": gain(ks[19], (KV_LORA,)),
        "w_kv_b": nrm(ks[20], (KV_LORA, B_HEADS * (QK_NOPE + V_HEAD)), KV_LORA ** -0.5),
        "k_gain": gain(ks[21], (QK_HEAD,)),
        "mem_norm": gain(ks[22], (D_MODEL,)),
        "mem_w_kv": nrm(ks[23], (DEPTH, D_MODEL, 2 * MEM_WIDTH), D_MODEL ** -0.5),
        "mem_q_gain": gain(ks[24], (DEPTH, MEM_HEAD_DIM)),
        "mem_k_gain": gain(ks[25], (DEPTH, MEM_HEAD_DIM)),
    }


def reference(x, mem, positions, a_norm, a_w_in, a_conv_w, a_conv_b, a_ig_bias, a_fg_bias, a_h_norm, a_w_out,
              b_norm, b_w_in, b_q_lat_norm, b_w_q_up, b_q_gain, b_w_out,
              kv_norm, w_kv_a, kv_lat_norm, w_kv_b, k_gain,
              mem_norm, mem_w_kv, mem_q_gain, mem_k_gain):
    mem_n = rmsnorm(mem, mem_norm)
    k_sh = None
    v_sh = None
    for layer in range(DEPTH):
        mk, mv = memory_kv(mem_n, mem_w_kv[layer], mem_k_gain[layer])
        if layer < N_A_LAYERS:
            x = mlstm_layer(x, a_norm[layer], a_w_in[layer], a_conv_w[layer], a_conv_b[layer],
                            a_ig_bias[layer], a_fg_bias[layer], a_h_norm[layer], a_w_out[layer],
                            mk, mv, mem_q_gain[layer])
        else:
            if layer == N_A_LAYERS:
                k_sh, v_sh = shared_kv(x, positions, kv_norm, w_kv_a, kv_lat_norm, w_kv_b, k_gain)
            j = layer - N_A_LAYERS
            x = mla_layer(x, positions, k_sh, v_sh, b_norm[j], b_w_in[j], b_q_lat_norm[j], b_w_q_up[j],
                          b_q_gain[j], b_w_out[j], mk, mv, mem_q_gain[layer])
    return x
```

```python
import bisect
import math
from contextlib import ExitStack

import numpy as np
import concourse.bass as bass
import concourse.mybir as mybir
from concourse.bass_utils import run_bass_kernel_spmd

F32 = mybir.dt.float32
BF16 = mybir.dt.bfloat16
I32 = mybir.dt.int32
AF = mybir.ActivationFunctionType
ALU = mybir.AluOpType
AX = mybir.AxisListType

D = 1024
NMEM = 256
TB = 256
EPS = 1e-6
A_IN = 6152
B_IN = 2432


class Res:
    __slots__ = ("name", "w", "r", "dead")

    def __init__(self, name):
        self.name = name
        self.w = None
        self.r = {}
        self.dead = False


def realloc(tl):
    old = tl.r
    new = Res(old.name)
    new.w = old.w
    new.r = old.r
    old.dead = True
    n = Tl(tl.t, old.name)
    n.r = new
    n.rs = [new]
    if hasattr(tl, "chan"):
        n.chan = tl.chan
    return n


class Chan:
    def __init__(self, sem):
        self.sem = sem
        self.val = 0
        self.name = "chan"

    def resolve(self, v):
        return v


class Eng:
    def __init__(self, name, obj, sem):
        self.name = name
        self.obj = obj
        self.sem = sem
        self.n = 0
        self.val = 0
        self.sig_idx = []
        self.sig_val = []
        self.last = None
        self.last_idx = 0
        self.seen = {}

    def resolve(self, idx):
        i = bisect.bisect_left(self.sig_idx, idx)
        if i < len(self.sig_idx):
            return self.sig_val[i]
        assert self.last_idx >= idx and self.last is not None
        self.last.then_inc(self.sem, 1)
        self.val += 1
        self.sig_idx.append(self.last_idx)
        self.sig_val.append(self.val)
        return self.val


class Tl:
    def __init__(self, t, name, nres=1):
        self.t = t
        self.r = Res(name)
        self.rs = [Res(f"{name}.{i}") for i in range(nres)] if nres > 1 else [self.r]

    def __getitem__(self, k):
        return self.t[k]


class Ctx:
    def __init__(self):
        self.nc = bass.Bass("TRN2", target_bir_lowering=False)
        self.es = ExitStack()
        nc = self.nc
        self.pe = Eng("pe", nc.tensor, self._sem("s_pe"))
        self.act = Eng("act", nc.scalar, self._sem("s_act"))
        self.dve = Eng("dve", nc.vector, self._sem("s_dve"))
        self.pool = Eng("pool", nc.gpsimd, self._sem("s_pool"))
        self.sp = Eng("sp", nc.sync, self._sem("s_sp"))
        self.engs = [self.pe, self.act, self.dve, self.pool, self.sp]
        self.chans = []
        self.nbank = 0
        self.banks = []
        self.bbanks = []
        self.nbb = 0
        self.ninst = 0
        self.nrot = 8

    def _sem(self, name):
        return self.es.enter_context(self.nc.semaphore(name))

    def chan(self):
        ch = Chan(self._sem(f"s_ch{len(self.chans)}"))
        self.chans.append(ch)
        return ch

    def sb(self, name, shape, dt, nres=1, es=None):
        self.nsb = getattr(self, "nsb", 0) + 1
        t = (es or self.es).enter_context(self.nc.sbuf_tensor(f"sb{self.nsb}_{name}", list(shape), dt))
        return Tl(t, name, nres)

    def init_psum(self, nf=8):
        for i in range(nf):
            t = self.es.enter_context(self.nc.psum_tensor(f"psf{i}", [128, 512], F32))
            self.banks.append(Tl(t, f"psf{i}"))

    def bank(self):
        i = self.nbank % self.nrot
        self.nbank += 1
        self.banks[i] = realloc(self.banks[i])
        return self.banks[i]

    def bbank(self):
        i = self.nbb % len(self.bbanks)
        self.nbb += 1
        self.bbanks[i] = realloc(self.bbanks[i])
        return self.bbanks[i]

    def issue(self, E, fn, R=(), W=(), chan=None):
        raw = {}
        war = {}

        def add(d, tok):
            o, x = tok
            if o not in d or d[o] < x:
                d[o] = x

        for r in R:
            assert not r.dead, f"stale pool handle {r.name}"
            if r.w is not None:
                add(raw, r.w)
        for w in W:
            assert not w.dead, f"stale pool handle {w.name}"
            if w.w is not None:
                add(raw, w.w)
            for o, x in w.r.items():
                add(war, (o, x))
        need = dict(raw)
        for o, x in war.items():
            if o not in need or need[o] < x:
                need[o] = x
        waits = []
        for o, x in need.items():
            if o is E and E is self.pe:
                continue
            v = o.resolve(x)
            if E.seen.get(o, 0) >= v:
                continue
            E.seen[o] = v
            waits.append((o.sem, v))
        if chan is not None:
            for sem, v in waits:
                E.obj.wait_ge(sem, v)
            inst = fn()
            chan.val += 16
            inst.then_inc(chan.sem, 16)
            tok = (chan, chan.val)
        else:
            for sem, v in waits[:-1]:
                E.obj.wait_ge(sem, v)
            inst = fn()
            if waits:
                inst._wait_ge(*waits[-1])
            E.n += 1
            E.last = inst
            E.last_idx = E.n
            tok = (E, E.n)
        self.ninst += 1
        for r in R:
            o, x = tok
            if o not in r.r or r.r[o] < x:
                r.r[o] = x
        for w in W:
            w.w = tok
            w.r = {}
        return inst

    def barrier(self):
        toks = []
        for F in self.engs:
            if F.last is not None:
                toks.append((F, F.sem, F.resolve(F.last_idx)))
        for ch in self.chans:
            if ch.val:
                toks.append((ch, ch.sem, ch.val))
        for E in self.engs:
            for o, sem, v in toks:
                if o is E:
                    continue
                if E.seen.get(o, 0) >= v:
                    continue
                E.seen[o] = v
                E.obj.wait_ge(sem, v)

    def mm(self, out, lhsT, rhs, start, stop, R, W):
        return self.issue(self.pe, lambda: self.nc.tensor.matmul(out, lhsT, rhs, start=start, stop=stop), R, W)

    def tr(self, out, in_, ident, R, W):
        return self.issue(self.pe, lambda: self.nc.tensor.transpose(out, in_, ident), R, W)

    def actf(self, out, in_, func, R, W, bias=None, scale=1.0, accum_out=None):
        kw = {}
        if bias is not None:
            kw["bias"] = bias
        if accum_out is not None:
            kw["accum_out"] = accum_out
        return self.issue(self.act, lambda: self.nc.scalar.activation(out=out, in_=in_, func=func, scale=scale, **kw), R, W)

    def tt(self, out, a, b, op, R, W, eng=None):
        E = eng or self.dve
        return self.issue(E, lambda: E.obj.tensor_tensor(out=out, in0=a, in1=b, op=op), R, W)

    def ts(self, out, a, s1, s2, op0, op1, R, W, eng=None):
        E = eng or self.dve
        if op1 is None:
            return self.issue(E, lambda: E.obj.tensor_scalar(out=out, in0=a, scalar1=s1, scalar2=None, op0=op0), R, W)
        return self.issue(E, lambda: E.obj.tensor_scalar(out=out, in0=a, scalar1=s1, scalar2=s2, op0=op0, op1=op1), R, W)

    def stt(self, out, in0, scalar, in1, op0, op1, R, W, eng=None):
        E = eng or self.dve
        return self.issue(E, lambda: E.obj.scalar_tensor_tensor(out=out, in0=in0, scalar=scalar, in1=in1, op0=op0, op1=op1), R, W)

    def cp(self, out, in_, R, W, eng=None):
        E = eng or self.dve
        return self.issue(E, lambda: E.obj.tensor_copy(out=out, in_=in_), R, W)

    def recip(self, out, in_, R, W):
        return self.issue(self.dve, lambda: self.nc.vector.reciprocal(out=out, in_=in_), R, W)

    def scan(self, out, d0, d1, init, op0, op1, R, W):
        return self.issue(self.dve, lambda: self.nc.vector.tensor_tensor_scan(out=out, data0=d0, data1=d1, initial=init, op0=op0, op1=op1), R, W)

    def memset(self, ap, v, W, eng=None):
        E = eng or self.dve
        return self.issue(E, lambda: E.obj.memset(ap, v), (), W)

    def dma(self, q, out, in_, R, W, chan):
        return self.issue(q, lambda: q.obj.dma_start(out=out, in_=in_), R, W, chan=chan)


def pack_w(W, groups):
    K = W.shape[0]
    kc = K // 128
    outs = []
    for (c0, wd) in groups:
        blk = W[:, c0:c0 + wd].reshape(kc, 128, wd).transpose(1, 0, 2).reshape(128, kc * wd)
        outs.append(blk)
    return np.ascontiguousarray(np.concatenate(outs, axis=1))


def group_offsets(kc, groups):
    offs = []
    o = 0
    for (_, wd) in groups:
        offs.append(o)
        o += kc * wd
    return offs, o


A_GROUPS = ([(i * 256, 256) for i in range(8)] +
            [(2048 + i * 256, 256) for i in range(4)] +
            [(3072 + i * 256, 256) for i in range(4)] +
            [(4096 + i * 256, 256) for i in range(4)] +
            [(5128 + i * 256, 256) for i in range(2)] +
            [(5640 + i * 256, 256) for i in range(2)] +
            [(5120, 8)])
A_OFFS, A_TOT = group_offsets(8, A_GROUPS)
OUT_GROUPS = [(i * 128, 128) for i in range(8)]
OUT_OFFS, OUT_TOT = group_offsets(12, OUT_GROUPS)
MKV_GROUPS = [(i * 256, 256) for i in range(4)]
MKV_OFFS, MKV_TOT = group_offsets(8, MKV_GROUPS)
B_GROUPS = ([(0, 256), (256, 128)] +
            [(384 + i * 256, 256) for i in range(4)] +
            [(1408 + i * 256, 256) for i in range(2)] +
            [(1920 + i * 256, 256) for i in range(2)])
B_OFFS, B_TOT = group_offsets(8, B_GROUPS)
QUP_GROUPS = [(h * 192, 192) for h in range(8)]
QUP_OFFS, QUP_TOT = group_offsets(3, QUP_GROUPS)
KVA_GROUPS = [(0, 256), (256, 64)]
KVA_OFFS, KVA_TOT = group_offsets(8, KVA_GROUPS)
KVB_GROUPS = [(0, 256), (256, 256), (512, 256), (768, 256), (1024, 256), (1280, 256), (1536, 256), (1792, 256)]
KVB_OFFS, KVB_TOT = group_offsets(2, KVB_GROUPS)

NCONST = 128 * 3 + 256 + 64 + 8
C_ID, C_MA, C_MB, C_ONE, C_RM, C_MISC = 0, 128, 256, 384, 640, 704


def make_consts():
    c = np.zeros((128, NCONST), np.float32)
    c[:, C_ID:C_ID + 128] = np.eye(128, dtype=np.float32)
    s = np.arange(128)[:, None]
    t = np.arange(128)[None, :]
    c[:, C_MA:C_MA + 128] = (s <= t).astype(np.float32)
    c[:, C_MB:C_MB + 128] = ((s // 64) <= (t // 64)).astype(np.float32)
    c[:, C_ONE:C_ONE + 256] = 1.0
    rm = np.zeros((64, 64), np.float32)
    for m in range(32):
        rm[m + 32, m] = -1.0
    for m in range(32, 64):
        rm[m - 32, m] = 1.0
    c[0:64, C_RM:C_RM + 64] = rm
    inv = (10000.0 ** (-np.arange(0, 64, 2, dtype=np.float32) / 64)).astype(np.float32)
    c[0:32, C_MISC] = inv
    c[32:64, C_MISC] = inv
    return c


class Prog:
    def __init__(self, T, nseq, phases):
        self.T = T
        self.nseq = nseq
        self.phases = phases
        self.c = Ctx()
        self.nc = self.c.nc
        self.nb = T // TB
        self.debug = False
        self.dbgd = {}

    def decl(self):
        nc, T, ns = self.nc, self.T, self.nseq
        d = {}

        def inp(name, shape, dt=F32):
            d[name] = nc.dram_tensor(name, list(shape), dt, kind="ExternalInput").ap()

        inp("xT", [ns, D, T])
        inp("memT", [ns, D, NMEM])
        inp("pos", [ns, 64, T], I32)
        inp("consts", [128, NCONST])
        inp("a_w_in", [2, 128, A_TOT])
        inp("a_w_out", [2, 128, OUT_TOT])
        inp("b_w_in", [2, 128, B_TOT])
        inp("b_w_qup", [2, 128, QUP_TOT])
        inp("b_w_out", [2, 128, OUT_TOT])
        inp("w_kva", [128, KVA_TOT])
        inp("w_kvb", [128, KVB_TOT])
        inp("mem_w_kv", [4, 128, MKV_TOT])
        inp("vecs", [128, NVEC])
        inp("rows", [1, NROW])
        d["outT"] = nc.dram_tensor("outT", [ns, D, T], F32, kind="ExternalOutput").ap()
        self.d = d
        self.dres = {k: Res("dram_" + k) for k in d}
        self.wbf = {}
        self.wbf_res = {}
        self.wbf_chan = {}
        for name in ("a_w_in", "a_w_out", "mem_w_kv", "w_kva", "w_kvb", "b_w_in", "b_w_qup", "b_w_out"):
            shp = list(d[name].shape)
            self.wbf[name] = nc.dram_tensor(name + "_bf", shp, BF16, kind="Internal").ap()

    def precast(self, order):
        c = self.c
        CH = 8192
        for (name, layer) in order:
            key = (name, layer)
            if key in self.wbf_res:
                continue
            self.wbf_res[key] = Res(f"wbf_{name}_{layer}")
            ch = c.chan()
            self.wbf_chan[key] = ch
            src = self.d[name]
            dst = self.wbf[name]
            tot = src.shape[-1]
            for o in range(0, tot, CH):
                n = min(CH, tot - o)
                if layer is None:
                    sa, da = src[:, o:o + n], dst[:, o:o + n]
                else:
                    sa, da = src[layer, :, o:o + n], dst[layer, :, o:o + n]
                c.dma(c.pool, da, sa, R=[self.dres[name]], W=[self.wbf_res[key]], chan=ch)


V_ANORM = 0
V_BNORM = 16
V_KVNORM = 32
V_MEMNORM = 40
V_CONVW = 48
V_CONVB = 176
V_MQG = 208
V_MKG = 212
V_QLATG = 216
V_KVLATG = 222
V_QG_NOPE = 224
V_QG_ROPE = 226
V_KG_NOPE = 228
V_KG_ROPE = 229
V_IB = 230
V_NFB = 232
NVEC = 240
NROW = 2048


def _P(cls):
    def deco(f):
        setattr(cls, f.__name__, f)
        return f
    return deco


NT = TB // 128


def rr(gens, width, stagger=0):
    pending = [g if isinstance(g, tuple) else (None, g) for g in gens]
    active = []
    rnd = 0
    while True:
        w = width if not stagger else min(width, 1 + rnd // stagger)
        i = 0
        while len(active) < w and i < len(pending):
            tag = pending[i][0]
            if tag is not None and any(t == tag for t, _ in active):
                i += 1
                continue
            active.append(pending.pop(i))
        if not active:
            assert not pending
            return
        for item in list(active):
            try:
                next(item[1])
            except StopIteration:
                active.remove(item)
        rnd += 1


def rr_gen(gens):
    active = list(gens)
    while active:
        for g in list(active):
            try:
                next(g)
            except StopIteration:
                active.remove(g)
        yield


@_P(Prog)
def alloc_common(self):
    c, T = self.c, self.T
    self.xT = c.sb("xT", [128, 8, T], F32, nres=self.nb * 8)
    self.xT.chan = c.chan()
    self.ochan = c.chan()
    self.cst = c.sb("cst", [128, NCONST], F32)
    self.cst.chan = c.chan()
    self.vecs = c.sb("vecs", [128, NVEC], F32)
    self.vecs.chan = c.chan()
    self.negfb = c.sb("negfb", [128, 2], F32)
    self.ident = c.sb("ident", [128, 128], BF16)
    self.ones = c.sb("ones", [128, 128], BF16)
    self.maskB = c.sb("maskB", [128, 128], BF16)
    self.lnsc = c.sb("lnsc", [128, 1], F32)
    self.mask16 = c.sb("mask16", [128, 128], F32)
    self.rmT = c.sb("rmT", [64, 64], BF16)
    self.xn = c.sb("xn", [128, 8, TB], BF16, nres=8)
    self.ymix = Tl(self.xn.t, "ymix")
    self.ymix.r = self.xn.rs[0]
    self.ymix.rs = self.xn.rs
    self.ymem = c.sb("ymem", [128, 4, TB], BF16, nres=4)
    self.mkT = c.sb("mkT", [128, 4, NMEM], BF16)
    self.mv = c.sb("mv", [128, 2, 512], BF16)
    self.wchans = [c.chan() for _ in range(9)]
    self.fchans = [c.chan() for _ in range(2)]


@_P(Prog)
def alloc_pools(self, es, nw, nfp, nhp, nsp):
    c = self.c
    self.wbufs = []
    for i in range(nw):
        w = c.sb(f"wbuf{i}", [128, 2048], BF16, es=es)
        w.chan = self.wchans[i]
        self.wbufs.append(w)
    self.nw = 0
    self.fp = [c.sb(f"fp{i}", [128, 260], F32, es=es) for i in range(nfp)]
    self.nfp = 0
    self.hp = [c.sb(f"hp{i}", [128, 256], BF16, es=es) for i in range(nhp)]
    self.nhp = 0
    self.spl = [c.sb(f"sp{i}", [128, 16], F32, es=es) for i in range(nsp)]
    self.nsp = 0


@_P(Prog)
def dbg(self, name, ap, R, shape, dt=F32):
    if not getattr(self, "debug", False) or name in self.dbgd:
        return
    c = self.c
    t = self.nc.dram_tensor("dbg_" + name, list(shape), dt, kind="ExternalOutput").ap()
    self.dbgd[name] = t
    c.dma(c.sp, t, ap, R=R, W=[Res("dbg")], chan=self.ochan)


@_P(Prog)
def fpool(self):
    i = self.nfp % len(self.fp)
    self.nfp += 1
    self.fp[i] = realloc(self.fp[i])
    return self.fp[i]


@_P(Prog)
def hpool(self):
    i = self.nhp % len(self.hp)
    self.nhp += 1
    self.hp[i] = realloc(self.hp[i])
    return self.hp[i]


@_P(Prog)
def spool(self):
    i = self.nsp % len(self.spl)
    self.nsp += 1
    self.spl[i] = realloc(self.spl[i])
    return self.spl[i]


@_P(Prog)
def load_consts(self):
    c, d = self.c, self.d
    c.dma(c.sp, self.cst[:, :], d["consts"][:, :], R=[self.dres["consts"]], W=[self.cst.r], chan=self.cst.chan)
    c.dma(c.sp, self.vecs[:, :], d["vecs"][:, :], R=[self.dres["vecs"]], W=[self.vecs.r], chan=self.vecs.chan)
    c.cp(self.ident[:, :], self.cst[:, C_ID:C_ID + 128], R=[self.cst.r], W=[self.ident.r])
    c.cp(self.ones[:, :], self.cst[:, C_ONE:C_ONE + 128], R=[self.cst.r], W=[self.ones.r])
    c.cp(self.maskB[:, :], self.cst[:, C_MB:C_MB + 128], R=[self.cst.r], W=[self.maskB.r])
    c.memset(self.lnsc[:, :], math.log(192 ** -0.5), W=[self.lnsc.r])
    c.ts(self.mask16[:, :], self.cst[:, C_MA:C_MA + 128], 0.0625, None, ALU.mult, None, R=[self.cst.r], W=[self.mask16.r])
    c.cp(self.rmT[:, :], self.cst[0:64, C_RM:C_RM + 64], R=[self.cst.r], W=[self.rmT.r])
    c.ts(self.negfb[:, :], self.vecs[:, V_NFB:V_NFB + 2], -1.0, None, ALU.mult, None, R=[self.vecs.r], W=[self.negfb.r])


@_P(Prog)
def load_seq(self, s):
    c, d = self.c, self.d
    for k in range(8):
        c.dma(c.sp, self.xT[:, k, :], d["xT"][s, k * 128:(k + 1) * 128, :], R=[self.dres["xT"]],
              W=self.xT.rs, chan=self.xT.chan)


@_P(Prog)
def memnorm(self, s):
    c, d = self.c, self.d
    ps = c.bank()
    mts = [self.fpool(), self.fpool()]
    for i in range(2):
        mts[i].chan = self.fchans[i]
    for k in range(8):
        mt = mts[k % 2]
        c.dma(c.sp, mt[:, 0:NMEM], d["memT"][s, k * 128:(k + 1) * 128, :], R=[self.dres["memT"]], W=[mt.r], chan=mt.chan)
        sq = self.hpool()
        c.actf(sq[:, :], mt[:, 0:NMEM], AF.Square, R=[mt.r], W=[sq.r])
        c.mm(ps[:, 0:NMEM], self.ones[:, :], sq[:, :], k == 0, k == 7, R=[sq.r, self.ones.r], W=[ps.r])
    rstd = self.rstd_from(ps[:, 0:NMEM], ps.r, 128, NMEM, 1.0 / D)
    for k in range(8):
        mt = mts[k % 2]
        c.dma(c.sp, mt[:, 0:NMEM], d["memT"][s, k * 128:(k + 1) * 128, :], R=[self.dres["memT"]], W=[mt.r], chan=mt.chan)
        c.stt(self.memn[:, k, :], mt[:, 0:NMEM], self.vecs[:, V_MEMNORM + k:V_MEMNORM + k + 1],
              rstd[:, 0:NMEM], ALU.mult, ALU.mult, R=[mt.r, rstd.r, self.vecs.r], W=self.memn.rs)


@_P(Prog)
def store_seq(self, s):
    c, d = self.c, self.d
    for k in range(8):
        c.dma(c.sp, d["outT"][s, k * 128:(k + 1) * 128, :], self.xT[:, k, :], R=self.xT.rs,
              W=[self.dres["outT"]], chan=self.ochan)


@_P(Prog)
def rstd_from(self, ps_ap, ps_res, P, n, inv_dim):
    c = self.c
    r = self.fpool()
    c.ts(r[0:P, 0:n], ps_ap, inv_dim, EPS, ALU.mult, ALU.add, R=[], W=[r.r, ps_res])
    c.actf(r[0:P, 0:n], r[0:P, 0:n], AF.Ln, R=[], W=[r.r])
    c.actf(r[0:P, 0:n], r[0:P, 0:n], AF.Exp, R=[], W=[r.r], scale=-0.5)
    return r


@_P(Prog)
def loadw(self, wname, layer, goff, kc, wd):
    c = self.c
    i = self.nw % len(self.wbufs)
    self.nw += 1
    self.wbufs[i] = realloc(self.wbufs[i])
    wb = self.wbufs[i]
    src = self.wbf[wname]
    n = kc * wd
    src_ap = src[layer, :, goff:goff + n] if layer is not None else src[:, goff:goff + n]
    q = c.pool if (getattr(self, "cur_seq", 0) > 0 and self.nw % 2) else c.sp
    c.dma(q, wb[:, 0:n], src_ap, R=[self.wbf_res[(wname, layer)]], W=[wb.r], chan=wb.chan)
    return wb, wb.t[:, 0:n].rearrange("p (k c) -> p k c", k=kc)


@_P(Prog)
def xnorm(self, b, gcol):
    c = self.c
    blk = slice(b * TB, (b + 1) * TB)
    ps = c.bank()
    for k in range(8):
        sq = self.hpool()
        c.actf(sq[:, :], self.xT[:, k, blk], AF.Square, R=[self.xT.rs[b * 8 + k]], W=[sq.r])
        c.mm(ps[:, 0:TB], self.ones[:, :], sq[:, :], k == 0, k == 7, R=[sq.r, self.ones.r], W=[ps.r])
    rstd = self.rstd_from(ps[:, 0:TB], ps.r, 128, TB, 1.0 / D)
    for k in range(8):
        c.stt(self.xn[:, k, :], self.xT[:, k, blk], self.vecs[:, gcol + k:gcol + k + 1], rstd[:, 0:TB],
              ALU.mult, ALU.mult, R=[self.xT.rs[b * 8 + k], rstd.r, self.vecs.r], W=[self.xn.rs[k]])


@_P(Prog)
def memkv(self, L):
    c = self.c
    for g in range(4):
        wb, wv = self.loadw("mem_w_kv", L, MKV_OFFS[g], 8, 256)
        if g < 2:
            for half in range(2):
                h = 2 * g + half
                ps = c.bank()
                for k in range(8):
                    c.mm(ps[:, 0:NMEM], wv[:, k, half * 128:(half + 1) * 128], self.memn[:, k, :], k == 0, k == 7,
                         R=[wb.r] + self.memn.rs, W=[ps.r])
                kf = self.fpool()
                c.actf(kf[:, 0:NMEM], ps[:, 0:NMEM], AF.Copy, R=[], W=[kf.r, ps.r])
                sq = self.hpool()
                c.actf(sq[:, :], kf[:, 0:NMEM], AF.Square, R=[kf.r], W=[sq.r])
                pss = c.bank()
                c.mm(pss[:, 0:NMEM], self.ones[:, :], sq[:, :], True, True, R=[sq.r, self.ones.r], W=[pss.r])
                rs = self.rstd_from(pss[:, 0:NMEM], pss.r, 128, NMEM, 1.0 / 128)
                c.stt(self.mkT[:, h, :], kf[:, 0:NMEM], self.vecs[:, V_MKG + L:V_MKG + L + 1], rs[:, 0:NMEM],
                      ALU.mult, ALU.mult, R=[kf.r, rs.r, self.vecs.r], W=[self.mkT.r])
        else:
            for mc in range(2):
                ps = c.bank()
                for k in range(8):
                    c.mm(ps[:, 0:256], self.memn[:, k, mc * 128:(mc + 1) * 128], wv[:, k, :], k == 0, k == 7,
                         R=[wb.r] + self.memn.rs, W=[ps.r])
                c.actf(self.mv[:, mc, (g - 2) * 256:(g - 1) * 256], ps[:, 0:256], AF.Copy, R=[], W=[self.mv.r, ps.r])


@_P(Prog)
def g_memattn(self, wq, wbq, wz, wbz, half, h, L):
    c = self.c
    cs = slice(half * 128, (half + 1) * 128)
    psq = c.bank()
    for k in range(8):
        c.mm(psq[:, 0:TB], wq[:, k, cs], self.xn[:, k, :], k == 0, k == 7, R=[wbq.r, self.xn.rs[k]], W=[psq.r])
    yield
    mqf = self.fpool()
    c.actf(mqf[:, 0:TB], psq[:, 0:TB], AF.Copy, R=[], W=[mqf.r, psq.r])
    yield
    sq = self.hpool()
    c.actf(sq[:, 0:TB], mqf[:, 0:TB], AF.Square, R=[mqf.r], W=[sq.r])
    psz = c.bank()
    for k in range(8):
        c.mm(psz[:, 0:TB], wz[:, k, cs], self.xn[:, k, :], k == 0, k == 7, R=[wbz.r, self.xn.rs[k]], W=[psz.r])
    yield
    pss = c.bank()
    c.mm(pss[:, 0:TB], self.ones[:, :], sq[:, 0:TB], True, True, R=[sq.r, self.ones.r], W=[pss.r])
    sz = self.fpool()
    c.actf(sz[:, 0:TB], psz[:, 0:TB], AF.Sigmoid, R=[], W=[sz.r, psz.r])
    yield
    g1 = self.fpool()
    c.tt(g1[:, 0:TB], psz[:, 0:TB], sz[:, 0:TB], ALU.mult, R=[sz.r], W=[g1.r, psz.r])
    yield
    rs = self.fpool()
    c.ts(rs[:, 0:TB], pss[:, 0:TB], 1.0 / 128, EPS, ALU.mult, ALU.add, R=[], W=[rs.r, pss.r])
    yield
    c.actf(rs[:, 0:TB], rs[:, 0:TB], AF.Ln, R=[], W=[rs.r])
    yield
    c.actf(rs[:, 0:TB], rs[:, 0:TB], AF.Exp, R=[], W=[rs.r], scale=-0.5)
    yield
    qn = self.hpool()
    c.stt(qn[:, 0:TB], mqf[:, 0:TB], self.vecs[:, V_MQG + L:V_MQG + L + 1], rs[:, 0:TB], ALU.mult, ALU.mult,
          R=[mqf.r, rs.r, self.vecs.r], W=[qn.r])
    yield
    pT = []
    for mc in range(2):
        pssc = c.bank()
        c.mm(pssc[:, 0:TB], self.mkT[:, h, mc * 128:(mc + 1) * 128], qn[:, 0:TB], True, True,
             R=[self.mkT.r, qn.r], W=[pssc.r])
        p = self.hpool()
        c.actf(p[:, 0:TB], pssc[:, 0:TB], AF.Exp, R=[], W=[p.r, pssc.r], scale=128 ** -0.5)
        pT.append(p)
        yield
    psy = c.bank()
    for mc in range(2):
        c.mm(psy[:, 0:TB], self.mv[:, mc, h * 128:(h + 1) * 128], pT[mc][:, 0:TB], mc == 0, mc == 1,
             R=[self.mv.r, pT[mc].r], W=[psy.r])
    psd = c.bank()
    for mc in range(2):
        c.mm(psd[:, 0:TB], self.ones[:, :], pT[mc][:, 0:TB], mc == 0, mc == 1, R=[self.ones.r, pT[mc].r], W=[psd.r])
    yield
    rden = self.fpool()
    c.recip(rden[:, 0:TB], psd[:, 0:TB], R=[], W=[rden.r, psd.r])
    yield
    y1 = self.fpool()
    c.tt(y1[:, 0:TB], psy[:, 0:TB], rden[:, 0:TB], ALU.mult, R=[rden.r], W=[y1.r, psy.r])
    yield
    c.tt(self.ymem[:, h, :], y1[:, 0:TB], g1[:, 0:TB], ALU.mult, R=[y1.r, g1.r], W=[self.ymem.rs[h]])


@_P(Prog)
def g_memgroup(self, wname, l, offq, offz, g2, L):
    wbq, wq = self.loadw(wname, l, offq, 8, 256)
    wbz, wz = self.loadw(wname, l, offz, 8, 256)
    yield
    yield from rr_gen([self.g_memattn(wq, wbq, wz, wbz, half, 2 * g2 + half, L) for half in range(2)])


@_P(Prog)
def g_wout(self, wname, l, b, jo):
    c = self.c
    blk = slice(b * TB, (b + 1) * TB)
    wb, wv = self.loadw(wname, l, OUT_OFFS[jo], 12, 128)
    yield
    ps = c.bank()
    for k in range(8):
        c.mm(ps[:, 0:TB], wv[:, k, :], self.ymix[:, k, :], k == 0, False, R=[wb.r, self.ymix.rs[k]], W=[ps.r])
    for k in range(4):
        c.mm(ps[:, 0:TB], wv[:, 8 + k, :], self.ymem[:, k, :], False, k == 3, R=[wb.r, self.ymem.rs[k]], W=[ps.r])
    yield
    c.tt(self.xT[:, jo, blk], self.xT[:, jo, blk], ps[:, 0:TB], ALU.add, R=[], W=[self.xT.rs[b * 8 + jo], ps.r])


@_P(Prog)
def wout(self, wname, l, b):
    rr([self.g_wout(wname, l, b, jo) for jo in range(8)], 3, stagger=1)


@_P(Prog)
def allocA(self, es):
    c = self.c
    self.alloc_pools(es, nw=9, nfp=26, nhp=16, nsp=16)
    self.qT = c.sb("qT", [128, 8, TB], BF16, nres=8, es=es)
    self.kT = c.sb("kT", [128, 8, TB], BF16, nres=8, es=es)
    self.ktok = c.sb("ktok", [128, NT, 1024], BF16, nres=NT, es=es)
    self.vaug = c.sb("vaug", [128, NT, 4, 258], BF16, nres=NT * 4, es=es)
    self.gate = c.sb("gate", [128, 8, TB], BF16, nres=8, es=es)
    self.memn = self.gate
    self.httok = [c.sb(f"httok{i}", [128, 1024], BF16, nres=4, es=es) for i in range(2)]
    self.hist = c.sb("hist", [128, 16, 3], F32, nres=16, es=es)
    self.C = c.sb("Cst", [128, 2, 4, 257], F32, nres=4, es=es)
    self.Cbf = [c.sb(f"Cbf{i}", [128, 2, 257], BF16, es=es) for i in range(4)]
    self.hg = c.sb("hg", [128, 1024], F32, es=es)
    self.hg.chan = c.chan()
    self.gt = c.sb("gt", [128, NT, 16], F32, es=es)
    self.carry = c.sb("carry", [4, 4], F32, es=es)
    self.wif = c.sb("wif", [128, 64], BF16, es=es)
    self.wif.chan = c.chan()
    self.dg = c.sb("dg", [4, NT, 4], F32, es=es)
    self.wat = c.sb("wat", [4, TB], F32, es=es)
    self.tht = c.sb("tht", [4, TB], F32, es=es)


@_P(Prog)
def g_qk_chunk(self, l, j, wb, wv, half):
    c = self.c
    xn = self.xn
    cw0 = V_CONVW + l * 64
    cb0 = V_CONVB + l * 16
    ps = c.bank()
    for k in range(8):
        c.mm(ps[:, 0:TB], wv[:, k, half * 128:(half + 1) * 128], xn[:, k, :], k == 0, k == 7, R=[wb.r, xn.rs[k]], W=[ps.r])
    yield
    cs = self.fpool()
    c.cp(cs[:, 0:3], self.hist[:, j, :], R=[self.hist.rs[j]], W=[cs.r])
    c.actf(cs[:, 3:3 + TB], ps[:, 0:TB], AF.Copy, R=[], W=[cs.r, ps.r])
    yield
    c.cp(self.hist[:, j, :], cs[:, TB:TB + 3], R=[cs.r], W=[self.hist.rs[j]])
    acc = self.fpool()
    wc = cw0 + j * 4
    c.ts(acc[:, 0:TB], cs[:, 3:3 + TB], self.vecs[:, wc + 3:wc + 4], self.vecs[:, cb0 + j:cb0 + j + 1],
         ALU.mult, ALU.add, R=[cs.r, self.vecs.r], W=[acc.r])
    yield
    for tap in (2, 1, 0):
        c.stt(acc[:, 0:TB], cs[:, tap:tap + TB], self.vecs[:, wc + tap:wc + tap + 1], acc[:, 0:TB],
              ALU.mult, ALU.add, R=[cs.r, self.vecs.r], W=[acc.r])
        yield
    sg = self.fpool()
    c.actf(sg[:, 0:TB], acc[:, 0:TB], AF.Sigmoid, R=[acc.r], W=[sg.r])
    yield
    if j < 8:
        c.tt(self.qT[:, j, :], acc[:, 0:TB], sg[:, 0:TB], ALU.mult, R=[acc.r, sg.r], W=[self.qT.rs[j]])
    else:
        c.tt(self.kT[:, j - 8, :], acc[:, 0:TB], sg[:, 0:TB], ALU.mult, R=[acc.r, sg.r], W=[self.kT.rs[j - 8]])


@_P(Prog)
def g_qk(self, l, g):
    wb, wv = self.loadw("a_w_in", l, A_OFFS[g], 8, 256)
    yield
    yield from rr_gen([self.g_qk_chunk(l, 2 * g + half, wb, wv, half) for half in range(2)])


@_P(Prog)
def g_oz_chunk(self, j, wo, wbo, wz, wbz, half):
    c, xn = self.c, self.xn
    cs_ = slice(half * 128, (half + 1) * 128)
    pso = c.bank()
    for k in range(8):
        c.mm(pso[:, 0:TB], wo[:, k, cs_], xn[:, k, :], k == 0, k == 7, R=[wbo.r, xn.rs[k]], W=[pso.r])
    yield
    so = self.fpool()
    c.actf(so[:, 0:TB], pso[:, 0:TB], AF.Sigmoid, R=[], W=[so.r, pso.r])
    psz = c.bank()
    for k in range(8):
        c.mm(psz[:, 0:TB], wz[:, k, cs_], xn[:, k, :], k == 0, k == 7, R=[wbz.r, xn.rs[k]], W=[psz.r])
    yield
    sz = self.fpool()
    c.actf(sz[:, 0:TB], psz[:, 0:TB], AF.Sigmoid, R=[], W=[sz.r, psz.r])
    yield
    g1 = self.fpool()
    c.tt(g1[:, 0:TB], psz[:, 0:TB], sz[:, 0:TB], ALU.mult, R=[sz.r], W=[g1.r, psz.r])
    yield
    c.tt(self.gate[:, j, :], g1[:, 0:TB], so[:, 0:TB], ALU.mult, R=[g1.r, so.r], W=[self.gate.rs[j]])


@_P(Prog)
def g_oz(self, l, g):
    wbo, wo = self.loadw("a_w_in", l, A_OFFS[12 + g], 8, 256)
    wbz, wz = self.loadw("a_w_in", l, A_OFFS[16 + g], 8, 256)
    yield
    yield from rr_gen([self.g_oz_chunk(2 * g + half, wo, wbo, wz, wbz, half) for half in range(2)])


@_P(Prog)
def g_v(self, l, hh):
    c, xn = self.c, self.xn
    wb, wv = self.loadw("a_w_in", l, A_OFFS[8 + hh], 8, 256)
    yield
    for tt_ in range(NT):
        tsl = slice(tt_ * 128, (tt_ + 1) * 128)
        ps = c.bank()
        for k in range(8):
            c.mm(ps[:, 0:256], xn[:, k, tsl], wv[:, k, :], k == 0, k == 7, R=[wb.r, xn.rs[k]], W=[ps.r])
        yield
        c.actf(self.vaug[:, tt_, hh, 0:256], ps[:, 0:256], AF.Copy, R=[self.gt.r],
               W=[self.vaug.rs[tt_ * 4 + hh], ps.r], scale=self.gt[:, tt_, hh:hh + 1])
        c.cp(self.vaug[:, tt_, hh, 256:257], self.gt[:, tt_, hh:hh + 1], R=[self.gt.r],
             W=[self.vaug.rs[tt_ * 4 + hh]])
        yield


@_P(Prog)
def g_ktrans(self, tt_, jj):
    c = self.c
    tsl = slice(tt_ * 128, (tt_ + 1) * 128)
    pb = c.bank()
    pbv = pb.t[:, :].bitcast(BF16)
    for j in range(4):
        c.tr(pbv[:, j * 128:(j + 1) * 128], self.kT[:, jj + j, tsl], self.ident[:, :],
             R=[self.kT.rs[jj + j], self.ident.r], W=[pb.r])
    yield
    c.actf(self.ktok[:, tt_, jj * 128:(jj + 4) * 128], pbv[:, 0:512], AF.Copy, R=[], W=[self.ktok.rs[tt_], pb.r])


@_P(Prog)
def g_gates(self, l):
    c, cst = self.c, self.cst
    xn = self.xn
    wif = self.wif.t[:, :].rearrange("p (k c) -> p k c", k=8)
    ones4 = cst[0:4, C_ONE:C_ONE + TB]
    I4 = cst[0:4, C_ID:C_ID + 4]
    Gi = c.bank()
    for k in range(8):
        c.mm(Gi[0:4, 0:TB], wif[:, k, 0:4], xn[:, k, :], k == 0, k == 7, R=[self.wif.r, xn.rs[k]], W=[Gi.r])
    Gf = c.bank()
    for k in range(8):
        c.mm(Gf[0:4, 0:TB], wif[:, k, 4:8], xn[:, k, :], k == 0, k == 7, R=[self.wif.r, xn.rs[k]], W=[Gf.r])
    yield
    it = self.fpool()
    c.actf(it[0:4, 0:TB], Gi[0:4, 0:TB], AF.Identity, R=[self.vecs.r], W=[it.r, Gi.r],
           bias=self.vecs[0:4, V_IB + l:V_IB + l + 1])
    et = self.fpool()
    c.actf(et[0:4, 0:TB], Gf[0:4, 0:TB], AF.Exp, R=[self.negfb.r], W=[et.r, Gf.r],
           bias=self.negfb[0:4, l:l + 1], scale=-1.0)
    yield
    lt = self.fpool()
    c.actf(lt[0:4, 0:TB], et[0:4, 0:TB], AF.Ln, R=[et.r], W=[lt.r], bias=1.0)
    yield
    Bn = self.fpool()
    c.scan(Bn[0:4, 0:TB], ones4, lt[0:4, 0:TB], self.carry[0:4, 0:1], ALU.mult, ALU.add,
           R=[cst.r, lt.r, self.carry.r], W=[Bn.r])
    yield
    At = self.fpool()
    c.tt(At[0:4, 0:TB], it[0:4, 0:TB], Bn[0:4, 0:TB], ALU.add, R=[it.r, Bn.r], W=[At.r])
    yield
    Mt = self.fpool()
    c.scan(Mt[0:4, 0:TB], ones4, At[0:4, 0:TB], self.carry[0:4, 1:2], ALU.mult, ALU.max,
           R=[cst.r, At.r, self.carry.r], W=[Mt.r])
    yield
    mp = self.spool()
    c.cp(mp[0:4, 0:1], self.carry[0:4, 1:2], R=[self.carry.r], W=[mp.r])
    for cc in range(1, NT):
        c.cp(mp[0:4, cc:cc + 1], Mt[0:4, cc * 128 - 1:cc * 128], R=[Mt.r], W=[mp.r])
    me = self.spool()
    for cc in range(NT):
        c.cp(me[0:4, cc:cc + 1], Mt[0:4, cc * 128 + 127:cc * 128 + 128], R=[Mt.r], W=[me.r])
    yield
    c.cp(self.carry[0:4, 0:1], Bn[0:4, TB - 1:TB], R=[Bn.r], W=[self.carry.r])
    c.cp(self.carry[0:4, 1:2], Mt[0:4, TB - 1:TB], R=[Mt.r], W=[self.carry.r])
    wat, tht = self.wat, self.tht
    for cc in range(NT):
        sl = slice(cc * 128, (cc + 1) * 128)
        c.ts(wat[0:4, sl], At[0:4, sl], me[0:4, cc:cc + 1], None, ALU.subtract, None, R=[At.r, me.r], W=[wat.r])
        c.ts(tht[0:4, sl], Bn[0:4, sl], me[0:4, cc:cc + 1], None, ALU.subtract, None, R=[Bn.r, me.r], W=[tht.r])
    yield
    c.actf(wat[0:4, 0:TB], wat[0:4, 0:TB], AF.Exp, R=[], W=[wat.r])
    c.actf(tht[0:4, 0:TB], tht[0:4, 0:TB], AF.Exp, R=[], W=[tht.r])
    gst = self.spool()
    c.tt(gst[0:4, 0:NT], mp[0:4, 0:NT], me[0:4, 0:NT], ALU.subtract, R=[mp.r, me.r], W=[gst.r])
    yield
    c.actf(gst[0:4, 0:NT], gst[0:4, 0:NT], AF.Exp, R=[], W=[gst.r])
    yield
    for cc in range(NT):
        c.ts(self.dg[0:4, cc, :], I4, gst[0:4, cc:cc + 1], None, ALU.mult, None, R=[cst.r, gst.r], W=[self.dg.r])


@_P(Prog)
def g_gtp(self):
    c, cst = self.c, self.cst
    I4 = cst[0:4, C_ID:C_ID + 4]
    gtp = c.bank()
    for cc in range(NT):
        sl = slice(cc * 128, (cc + 1) * 128)
        c.mm(gtp[:, cc * 12:cc * 12 + 4], self.wat[0:4, sl], I4, True, True, R=[self.wat.r, cst.r], W=[gtp.r])
        c.mm(gtp[:, cc * 12 + 4:cc * 12 + 8], self.tht[0:4, sl], I4, True, True, R=[self.tht.r, cst.r], W=[gtp.r])
        c.mm(gtp[:, cc * 12 + 8:cc * 12 + 12], cst[0:4, C_ONE:C_ONE + 128], self.dg[0:4, cc, :], True, True,
             R=[self.dg.r, cst.r], W=[gtp.r])
    yield
    c.cp(self.gt[:, :, 0:12], gtp[:, 0:NT * 12].rearrange("p (c t) -> p c t", t=12), R=[], W=[self.gt.r, gtp.r])
    c.ts(self.gt[:, :, 12:16], self.gt[:, :, 8:12], 0.0625, None, ALU.mult, None, R=[], W=[self.gt.r])


@_P(Prog)
def g_mlstm_head(self, tt_, h, ht):
    c = self.c
    tsl = slice(tt_ * 128, (tt_ + 1) * 128)
    sps = c.bank()
    for dd in range(2):
        c.mm(sps[:, 0:128], self.kT[:, 2 * h + dd, tsl], self.qT[:, 2 * h + dd, tsl], dd == 0, dd == 1,
             R=[self.kT.rs[2 * h + dd], self.qT.rs[2 * h + dd]], W=[sps.r])
    cb = self.Cbf[h]
    gsc = self.gt[:, tt_, 8 + h:9 + h]
    for dd in range(2):
        c.actf(cb[:, dd, :], self.C[:, dd, h, :], AF.Copy, R=[self.C.rs[h], self.gt.r], W=[cb.r],
               scale=self.gt[:, tt_, 12 + h:13 + h])
    yield
    p0 = self.hpool()
    c.tt(p0[:, 0:128], sps[:, 0:128], self.mask16[:, :], ALU.mult, R=[self.mask16.r], W=[p0.r, sps.r])
    va = self.vaug[:, tt_, h, 0:257]
    var = self.vaug.rs[tt_ * 4 + h]
    dps0 = c.bank()
    c.mm(dps0[:, 0:257], self.ktok[:, tt_, (2 * h) * 128:(2 * h + 1) * 128], va, True, True,
         R=[self.ktok.rs[tt_], var], W=[dps0.r])
    yield
    nps = c.bank()
    c.mm(nps[:, 0:257], p0[:, 0:128], va, True, False, R=[p0.r, var], W=[nps.r])
    for dd in range(2):
        c.mm(nps[:, 0:257], self.qT[:, 2 * h + dd, tsl], cb[:, dd, :], False, dd == 1,
             R=[self.qT.rs[2 * h + dd], cb.r], W=[nps.r])
    c.stt(self.C[:, 0, h, :], self.C[:, 0, h, :], gsc, dps0[:, 0:257], ALU.mult, ALU.add,
          R=[self.gt.r], W=[self.C.rs[h], dps0.r])
    yield
    dps1 = c.bank()
    c.mm(dps1[:, 0:257], self.ktok[:, tt_, (2 * h + 1) * 128:(2 * h + 2) * 128], va, True, True,
         R=[self.ktok.rs[tt_], var], W=[dps1.r])
    sm = self.spool()
    c.cp(sm[:, 4:5], nps[:, 256:257], R=[], W=[sm.r, nps.r])
    yield
    c.stt(sm[:, 5:6], sm[:, 4:5], -1.0, sm[:, 4:5], ALU.mult, ALU.max, R=[], W=[sm.r])
    yield
    c.stt(self.C[:, 1, h, :], self.C[:, 1, h, :], gsc, dps1[:, 0:257], ALU.mult, ALU.add,
          R=[self.gt.r], W=[self.C.rs[h], dps1.r])
    yield
    c.tt(sm[:, 0:1], sm[:, 5:6], self.gt[:, tt_, 4 + h:5 + h], ALU.max, R=[self.gt.r], W=[sm.r])
    yield
    c.recip(sm[:, 1:2], sm[:, 0:1], R=[], W=[sm.r])
    yield
    a = self.fpool()
    c.actf(a[:, 0:256], nps[:, 0:256], AF.Copy, R=[sm.r], W=[a.r, nps.r], scale=sm[:, 1:2])
    yield
    junk = self.hpool()
    c.actf(junk[:, 0:256], a[:, 0:256], AF.Square, R=[a.r], W=[junk.r, sm.r], accum_out=sm[:, 2:3])
    yield
    c.ts(sm[:, 3:4], sm[:, 2:3], 1.0 / 256, EPS, ALU.mult, ALU.add, R=[], W=[sm.r])
    yield
    c.actf(sm[:, 3:4], sm[:, 3:4], AF.Ln, R=[], W=[sm.r])
    yield
    c.actf(sm[:, 3:4], sm[:, 3:4], AF.Exp, R=[], W=[sm.r], scale=-0.5)
    yield
    c.stt(ht[:, h * 256:(h + 1) * 256], a[:, 0:256], sm[:, 3:4], self.hg[:, h * 256:(h + 1) * 256],
          ALU.mult, ALU.mult, R=[a.r, sm.r, self.hg.r], W=[ht.rs[h]])


@_P(Prog)
def g_httrans(self, tt_, jj, ht):
    c = self.c
    tsl = slice(tt_ * 128, (tt_ + 1) * 128)
    pb = c.bank()
    pbv = pb.t[:, :].bitcast(BF16)
    for j in range(4):
        c.tr(pbv[:, j * 128:(j + 1) * 128], ht[:, (jj + j) * 128:(jj + j + 1) * 128], self.ident[:, :],
             R=[ht.rs[(jj + j) // 2], self.ident.r], W=[pb.r])
    yield
    c.tt(self.ymix[:, jj:jj + 4, tsl], pbv[:, 0:512].rearrange("p (j t) -> p j t", t=128),
         self.gate[:, jj:jj + 4, tsl], ALU.mult, R=[self.gate.rs[jj + i] for i in range(4)],
         W=[self.ymix.rs[jj + i] for i in range(4)] + [pb.r])


@_P(Prog)
def layerA(self, s, l):
    c, d = self.c, self.d
    L = l
    c.dma(c.sp, self.hg[:, :], d["rows"][0:1, l * 1024:(l + 1) * 1024].to_broadcast([128, 1024]),
          R=[self.dres["rows"]], W=[self.hg.r], chan=self.hg.chan)
    c.dma(c.sp, self.wif[:, :], self.wbf["a_w_in"][l, :, A_OFFS[24]:A_OFFS[24] + 64],
          R=[self.wbf_res[("a_w_in", l)]], W=[self.wif.r], chan=self.wif.chan)
    c.memset(self.hist[:, :, :], 0.0, W=self.hist.rs)
    c.memset(self.C[:, :, :, :], 0.0, W=self.C.rs)
    c.memset(self.carry[:, :], 0.0, W=[self.carry.r])
    self.memnorm(s)
    self.memkv(L)
    for b in range(self.nb):
        if b == 0:
            self.xnorm(b, V_ANORM + l * 8)
        items = [self.g_gates(l)]
        for g in range(4):
            items += [self.g_qk(l, g), self.g_oz(l, g)]
        rr(items, 3, stagger=4)
        rr([self.g_gtp()], 1)
        mg = [self.g_memgroup("a_w_in", l, A_OFFS[20 + g2], A_OFFS[22 + g2], g2, L) for g2 in range(2)]
        items = [self.g_qk(l, 4), ("m", mg[0]), self.g_v(l, 0), self.g_qk(l, 5), self.g_v(l, 1), self.g_qk(l, 6),
                 ("m", mg[1]), self.g_v(l, 2), self.g_qk(l, 7), self.g_v(l, 3)]
        rr(items, 3, stagger=4)
        rr([self.g_ktrans(tt_, jj) for tt_ in range(NT) for jj in (0, 4)], 2)
        hoist = False
        if hoist:
            self.xnorm(b + 1, V_ANORM + l * 8)
        for tt_ in range(NT):
            ht = self.httok[tt_ % 2]
            rr([self.g_mlstm_head(tt_, h, ht) for h in range(4)], 4, stagger=4)
            rr([self.g_httrans(tt_, jj, ht) for jj in (0, 4)], 2)
        self.dbg("xn", self.xn[:, :, :], self.xn.rs, [128, 8, TB], BF16)
        self.dbg("qT", self.qT[:, :, :], self.qT.rs, [128, 8, TB], BF16)
        self.dbg("kT", self.kT[:, :, :], self.kT.rs, [128, 8, TB], BF16)
        self.dbg("ktok", self.ktok[:, :, :], self.ktok.rs, [128, NT, 1024], BF16)
        self.dbg("gt", self.gt[:, :, :], [self.gt.r], [128, NT, 16])
        self.dbg("vaug", self.vaug[:, :, :, :], self.vaug.rs, [128, NT, 4, 258], BF16)
        self.dbg("gate", self.gate[:, :, :], self.gate.rs, [128, 8, TB], BF16)
        self.dbg("ymix", self.ymix[:, :, :], self.ymix.rs, [128, 8, TB], BF16)
        self.dbg("httok", self.httok[(NT - 1) % 2][:, :], self.httok[(NT - 1) % 2].rs, [128, 1024], BF16)
        self.dbg("Cst", self.C[:, :, :, :], self.C.rs, [128, 2, 4, 257])
        self.wout("a_w_out", l, b)
        if b + 1 < self.nb and not hoist:
            self.xnorm(b + 1, V_ANORM + l * 8)


@_P(Prog)
def allocB(self, es):
    c, T = self.c, self.T
    ntile = T // 128
    self.alloc_pools(es, nw=4, nfp=16, nhp=10, nsp=12)
    self.K0T = c.sb("K0T", [128, 8, T], BF16, nres=self.nb * 8, es=es)
    self.RT = c.sb("RT", [128, T], BF16, nres=self.nb, es=es)
    self.Vtok = c.sb("Vtok", [128, ntile, 1024], BF16, nres=self.nb * 4, es=es)
    self.rstdk = c.sb("rstdk", [128, ntile, 8], F32, nres=self.nb, es=es)
    self.qn = c.sb("qn", [128, 8, TB], BF16, nres=8, es=es)
    self.qr = c.sb("qr", [128, 8, TB], BF16, nres=8, es=es)
    self.szt = c.sb("szt", [128, 8, TB], BF16, nres=8, es=es)
    self.memn = self.szt
    self.cosT = c.sb("cosT", [64, TB], F32, es=es)
    self.sinT = c.sb("sinT", [64, TB], F32, es=es)
    self.posb = c.sb("posb", [64, TB], I32, es=es)
    self.posb.chan = c.chan()
    self.cn = c.sb("cn", [128, 2, TB], BF16, es=es)
    self.sqpe = c.sb("sqpe", [64, TB], BF16, es=es)
    self.qln = c.sb("qln", [128, 3, TB], BF16, es=es)
    self.invs = c.sb("invs", [64, 1], F32, es=es)


@_P(Prog)
def ropetab(self, s, b):
    c, d = self.c, self.d
    blk = slice(b * TB, (b + 1) * TB)
    c.dma(c.sp, self.posb[:, :], d["pos"][s, :, blk], R=[self.dres["pos"]], W=[self.posb.r], chan=self.posb.chan)
    pf = self.fpool()
    c.cp(pf[0:64, 0:TB], self.posb[:, :], R=[self.posb.r], W=[pf.r])
    u = self.fpool()
    c.ts(u[0:64, 0:TB], pf[0:64, 0:TB], self.invs[0:64, 0:1], None, ALU.mult, None, R=[pf.r, self.invs.r], W=[u.r])
    pi_ = self.fpool()
    piv = pi_.t[0:64, 0:TB].bitcast(I32)
    c.cp(piv, u[0:64, 0:TB], R=[u.r], W=[pi_.r])
    nf = self.fpool()
    c.cp(nf[0:64, 0:TB], piv, R=[pi_.r], W=[nf.r])
    f = self.fpool()
    c.tt(f[0:64, 0:TB], u[0:64, 0:TB], nf[0:64, 0:TB], ALU.subtract, R=[u.r, nf.r], W=[f.r])
    s1 = self.fpool()
    c.actf(s1[0:64, 0:TB], f[0:64, 0:TB], AF.Sin, R=[f.r], W=[s1.r], scale=math.pi)
    s2 = self.fpool()
    c.actf(s2[0:64, 0:TB], f[0:64, 0:TB], AF.Sin, R=[f.r], W=[s2.r], scale=0.5 * math.pi)
    c1 = self.fpool()
    c.tt(c1[0:64, 0:TB], s2[0:64, 0:TB], s2[0:64, 0:TB], ALU.mult, R=[s2.r], W=[c1.r])
    c.ts(c1[0:64, 0:TB], c1[0:64, 0:TB], -2.0, 1.0, ALU.mult, ALU.add, R=[], W=[c1.r])
    c.stt(self.sinT[:, :], s1[0:64, 0:TB], 2.0, c1[0:64, 0:TB], ALU.mult, ALU.mult, R=[s1.r, c1.r], W=[self.sinT.r])
    c.tt(c1[0:64, 0:TB], s1[0:64, 0:TB], s1[0:64, 0:TB], ALU.mult, R=[s1.r], W=[c1.r])
    c.ts(self.cosT[:, :], c1[0:64, 0:TB], -2.0, 1.0, ALU.mult, ALU.add, R=[c1.r], W=[self.cosT.r])


@_P(Prog)
def g_rope(self, src, gcol, dst_ap, dst_res, rstd=None):
    c = self.c
    g = self.fpool()
    if rstd is None:
        c.ts(g[0:64, 0:TB], src[0:64, 0:TB], self.vecs[0:64, gcol:gcol + 1], None, ALU.mult, None,
             R=[src.r, self.vecs.r], W=[g.r])
    else:
        c.stt(g[0:64, 0:TB], src[0:64, 0:TB], self.vecs[0:64, gcol:gcol + 1], rstd[0:64, 0:TB], ALU.mult, ALU.mult,
              R=[src.r, self.vecs.r, rstd.r], W=[g.r])
    yield
    ghi = self.hpool()
    c.actf(ghi[0:64, 0:TB], g[0:64, 0:TB], AF.Copy, R=[g.r], W=[ghi.r])
    yield
    glo_f = self.fpool()
    c.tt(glo_f[0:64, 0:TB], g[0:64, 0:TB], ghi[0:64, 0:TB], ALU.subtract, R=[g.r, ghi.r], W=[glo_f.r])
    t1 = self.fpool()
    c.tt(t1[0:64, 0:TB], g[0:64, 0:TB], self.cosT[:, :], ALU.mult, R=[g.r, self.cosT.r], W=[t1.r])
    yield
    glo = self.hpool()
    c.actf(glo[0:64, 0:TB], glo_f[0:64, 0:TB], AF.Copy, R=[glo_f.r], W=[glo.r])
    yield
    pr = c.bank()
    c.mm(pr[0:64, 0:TB], self.rmT[0:64, :], ghi[0:64, 0:TB], True, False, R=[self.rmT.r, ghi.r], W=[pr.r])
    c.mm(pr[0:64, 0:TB], self.rmT[0:64, :], glo[0:64, 0:TB], False, True, R=[self.rmT.r, glo.r], W=[pr.r])
    yield
    t2 = self.fpool()
    c.tt(t2[0:64, 0:TB], pr[0:64, 0:TB], self.sinT[:, :], ALU.mult, R=[self.sinT.r], W=[t2.r, pr.r])
    yield
    c.tt(dst_ap, t1[0:64, 0:TB], t2[0:64, 0:TB], ALU.add, R=[t1.r, t2.r], W=[dst_res])


@_P(Prog)
def setupB(self):
    c, cst = self.c, self.cst
    c.memset(self.RT[64:128, :], 0.0, W=self.RT.rs)
    c.memset(self.qr[64:128, :, :], 0.0, W=self.qr.rs)
    c.ts(self.invs[:, :], cst[0:64, C_MISC:C_MISC + 1], 1.0 / (2.0 * math.pi), None, ALU.mult, None,
         R=[cst.r], W=[self.invs.r])


@_P(Prog)
def sharedkv(self, s):
    c, cst = self.c, self.cst
    for b in range(self.nb):
        blk = slice(b * TB, (b + 1) * TB)
        self.xnorm(b, V_KVNORM)
        xn = self.xn
        self.ropetab(s, b)
        wb, wv = self.loadw("w_kva", None, KVA_OFFS[0], 8, 256)
        cf = []
        pss = c.banks[4]
        for half in range(2):
            ps = c.bank()
            for k in range(8):
                c.mm(ps[:, 0:TB], wv[:, k, half * 128:(half + 1) * 128], xn[:, k, :], k == 0, k == 7,
                     R=[wb.r, xn.rs[k]], W=[ps.r])
            f = self.fpool()
            c.actf(f[:, 0:TB], ps[:, 0:TB], AF.Copy, R=[], W=[f.r, ps.r])
            sq = self.hpool()
            c.actf(sq[:, 0:TB], f[:, 0:TB], AF.Square, R=[f.r], W=[sq.r])
            c.mm(pss[:, 0:TB], self.ones[:, :], sq[:, 0:TB], half == 0, half == 1, R=[sq.r, self.ones.r], W=[pss.r])
            cf.append(f)
        rs = self.rstd_from(pss[:, 0:TB], pss.r, 128, TB, 1.0 / 256)
        for half in range(2):
            c.stt(self.cn[:, half, :], cf[half][:, 0:TB], self.vecs[:, V_KVLATG + half:V_KVLATG + half + 1], rs[:, 0:TB],
                  ALU.mult, ALU.mult, R=[cf[half].r, rs.r, self.vecs.r], W=[self.cn.r])
        wb, wv = self.loadw("w_kva", None, KVA_OFFS[1], 8, 64)
        ps = c.bank()
        for k in range(8):
            c.mm(ps[0:64, 0:TB], wv[:, k, :], xn[:, k, :], k == 0, k == 7, R=[wb.r, xn.rs[k]], W=[ps.r])
        kpf = self.fpool()
        c.actf(kpf[0:64, 0:TB], ps[0:64, 0:TB], AF.Copy, R=[], W=[kpf.r, ps.r])
        c.actf(self.sqpe[:, :], kpf[0:64, 0:TB], AF.Square, R=[kpf.r], W=[self.sqpe.r])
        rr([self.g_rope(kpf, V_KG_ROPE, self.RT[0:64, blk], self.RT.rs[b])], 1)
        ssq = c.banks[5]
        rr([self.g_knope(b, g, ssq) for g in range(4)], 2)
        rk = self.spool()
        c.ts(rk[:, 0:NT * 8], ssq[:, 0:NT * 8], 1.0 / 192, EPS, ALU.mult, ALU.add, R=[], W=[rk.r, ssq.r])
        c.actf(rk[:, 0:NT * 8], rk[:, 0:NT * 8], AF.Ln, R=[], W=[rk.r])
        c.actf(rk[:, 0:NT * 8], rk[:, 0:NT * 8], AF.Exp, R=[self.lnsc.r], W=[rk.r], scale=-0.5, bias=self.lnsc[:, 0:1])
        c.cp(self.rstdk[:, b * NT:(b + 1) * NT, :], rk[:, 0:NT * 8].rearrange("p (t h) -> p t h", h=8), R=[rk.r],
             W=[self.rstdk.rs[b]])
        rr([self.g_vgroup(b, g) for g in range(4)], 2)


@_P(Prog)
def g_knope(self, b, g, ssq):
    c = self.c
    blk = slice(b * TB, (b + 1) * TB)
    wb, wv = self.loadw("w_kvb", None, KVB_OFFS[g], 2, 256)
    yield
    for half in range(2):
        h = 2 * g + half
        ps = c.bank()
        for kc in range(2):
            c.mm(ps[:, 0:TB], wv[:, kc, half * 128:(half + 1) * 128], self.cn[:, kc, :], kc == 0, kc == 1,
                 R=[wb.r, self.cn.r], W=[ps.r])
        yield
        c.ts(self.K0T[:, h, blk], ps[:, 0:TB], self.vecs[:, V_KG_NOPE:V_KG_NOPE + 1], None, ALU.mult, None,
             R=[self.vecs.r], W=[self.K0T.rs[b * 8 + h], ps.r])
        yield
        sq = self.hpool()
        c.actf(sq[:, 0:TB], ps[:, 0:TB], AF.Square, R=[], W=[sq.r, ps.r])
        yield
        for tt_ in range(NT):
            tsl = slice(tt_ * 128, (tt_ + 1) * 128)
            col = tt_ * 8 + h
            c.mm(ssq[:, col:col + 1], sq[:, tsl], self.ones[:, 0:1], True, False, R=[sq.r, self.ones.r], W=[ssq.r])
            c.mm(ssq[:, col:col + 1], self.sqpe[0:64, tsl], self.ones[0:64, 0:1], False, True,
                 R=[self.sqpe.r, self.ones.r], W=[ssq.r])
        yield


@_P(Prog)
def g_vgroup(self, b, g):
    c = self.c
    wb, wv = self.loadw("w_kvb", None, KVB_OFFS[4 + g], 2, 256)
    yield
    for tt_ in range(NT):
        tsl = slice(tt_ * 128, (tt_ + 1) * 128)
        ps = c.bank()
        for kc in range(2):
            c.mm(ps[:, 0:256], self.cn[:, kc, tsl], wv[:, kc, :], kc == 0, kc == 1, R=[wb.r, self.cn.r], W=[ps.r])
        yield
        c.actf(self.Vtok[:, b * NT + tt_, g * 256:(g + 1) * 256], ps[:, 0:256], AF.Copy, R=[],
               W=[self.Vtok.rs[b * 4 + g], ps.r])
        yield


@_P(Prog)
def g_qhead(self, j, h):
    c = self.c
    wb, wv = self.loadw("b_w_qup", j, QUP_OFFS[h], 3, 192)
    yield
    psn = c.bank()
    for kc in range(3):
        c.mm(psn[:, 0:TB], wv[:, kc, 0:128], self.qln[:, kc, :], kc == 0, kc == 2, R=[wb.r, self.qln.r], W=[psn.r])
    psr = c.bank()
    for kc in range(3):
        c.mm(psr[0:64, 0:TB], wv[:, kc, 128:192], self.qln[:, kc, :], kc == 0, kc == 2, R=[wb.r, self.qln.r],
             W=[psr.r])
    yield
    qnf = self.fpool()
    c.actf(qnf[:, 0:TB], psn[:, 0:TB], AF.Copy, R=[], W=[qnf.r, psn.r])
    qrf = self.fpool()
    c.actf(qrf[0:64, 0:TB], psr[0:64, 0:TB], AF.Copy, R=[], W=[qrf.r, psr.r])
    yield
    sqn = self.hpool()
    c.actf(sqn[:, 0:TB], qnf[:, 0:TB], AF.Square, R=[qnf.r], W=[sqn.r])
    sqr = self.hpool()
    c.actf(sqr[0:64, 0:TB], qrf[0:64, 0:TB], AF.Square, R=[qrf.r], W=[sqr.r])
    yield
    ps2 = c.bank()
    c.mm(ps2[:, 0:TB], self.ones[:, :], sqn[:, 0:TB], True, False, R=[sqn.r, self.ones.r], W=[ps2.r])
    c.mm(ps2[:, 0:TB], self.ones[0:64, :], sqr[0:64, 0:TB], False, True, R=[sqr.r, self.ones.r], W=[ps2.r])
    yield
    rq = self.fpool()
    c.ts(rq[:, 0:TB], ps2[:, 0:TB], 1.0 / 192, EPS, ALU.mult, ALU.add, R=[], W=[rq.r, ps2.r])
    yield
    c.actf(rq[:, 0:TB], rq[:, 0:TB], AF.Ln, R=[], W=[rq.r])
    yield
    c.actf(rq[:, 0:TB], rq[:, 0:TB], AF.Exp, R=[], W=[rq.r], scale=-0.5)
    yield
    c.stt(self.qn[:, h, :], qnf[:, 0:TB], self.vecs[:, V_QG_NOPE + j:V_QG_NOPE + j + 1], rq[:, 0:TB],
          ALU.mult, ALU.mult, R=[qnf.r, rq.r, self.vecs.r], W=[self.qn.rs[h]])
    yield from self.g_rope(qrf, V_QG_ROPE + j, self.qr[0:64, h, :], self.qr.rs[h], rstd=rq)


@_P(Prog)
def g_zgroup(self, j, g):
    c, xn = self.c, self.xn
    wb, wv = self.loadw("b_w_in", j, B_OFFS[2 + g], 8, 256)
    yield
    for half in range(2):
        jz = 2 * g + half
        ps = c.bank()
        for k in range(8):
            c.mm(ps[:, 0:TB], wv[:, k, half * 128:(half + 1) * 128], xn[:, k, :], k == 0, k == 7,
                 R=[wb.r, xn.rs[k]], W=[ps.r])
        yield
        sg = self.fpool()
        c.actf(sg[:, 0:TB], ps[:, 0:TB], AF.Sigmoid, R=[], W=[sg.r, ps.r])
        yield
        c.tt(self.szt[:, jz, :], ps[:, 0:TB], sg[:, 0:TB], ALU.mult, R=[sg.r], W=[self.szt.rs[jz], ps.r])
        yield


@_P(Prog)
def g_attn_head(self, b, h, par):
    c = self.c
    nkt = NT * (b + 1)
    po = c.banks[4 + 2 * par]
    pd = c.banks[5 + 2 * par]
    pend = None
    for kt in range(nkt + 1):
        cur = None
        if kt < nkt:
            i = kt - NT * b
            q0 = 0 if i <= 0 else i * 128
            N = TB - q0
            kb = kt // NT
            ksl = slice(kt * 128, (kt + 1) * 128)
            pss_ = c.bank()
            c.mm(pss_[:, 0:N], self.K0T[:, h, ksl], self.qn[:, h, q0:TB], True, False,
                 R=[self.K0T.rs[kb * 8 + h], self.qn.rs[h]], W=[pss_.r])
            c.mm(pss_[:, 0:N], self.RT[:, ksl], self.qr[:, h, q0:TB], False, True,
                 R=[self.RT.rs[kb], self.qr.rs[h]], W=[pss_.r])
            p = self.hpool()
            c.actf(p[:, 0:N], pss_[:, 0:N], AF.Exp, R=[self.rstdk.rs[kb]], W=[p.r, pss_.r],
                   scale=self.rstdk[:, kt, h:h + 1])
            if i >= 0:
                c.tt(p[:, 0:128], p[:, 0:128], self.maskB[:, :], ALU.mult, R=[self.maskB.r], W=[p.r])
            cur = (kt, p, q0, N, kb)
        if pend is not None:
            kt_, p_, q0_, N_, kb_ = pend
            c.mm(po[:, q0_:TB], self.Vtok[:, kt_, h * 128:(h + 1) * 128], p_[:, 0:N_], kt_ == 0,
                 kt_ == nkt - 1, R=[self.Vtok.rs[kb_ * 4 + h // 2], p_.r], W=[po.r])
            c.mm(pd[:, q0_:TB], self.ones[:, :], p_[:, 0:N_], kt_ == 0, kt_ == nkt - 1,
                 R=[self.ones.r, p_.r], W=[pd.r])
        pend = cur
        yield
    rden = self.fpool()
    c.recip(rden[:, 0:TB], pd[:, 0:TB], R=[], W=[rden.r, pd.r])
    yield
    y1 = self.fpool()
    c.tt(y1[:, 0:TB], po[:, 0:TB], rden[:, 0:TB], ALU.mult, R=[rden.r], W=[y1.r, po.r])
    yield
    c.tt(self.ymix[:, h, :], y1[:, 0:TB], self.szt[:, h, :], ALU.mult, R=[y1.r, self.szt.rs[h]], W=[self.ymix.rs[h]])


@_P(Prog)
def layerB(self, s, j):
    c, cst = self.c, self.cst
    L = 2 + j
    self.memnorm(s)
    self.memkv(L)
    for b in range(self.nb):
        c.nrot = 8
        self.xnorm(b, V_BNORM + 8 * j)
        self.ropetab(s, b)
        xn = self.xn
        pss = c.bank()
        qlf = []
        for ci in range(3):
            if ci == 0:
                wb, wv = self.loadw("b_w_in", j, B_OFFS[0], 8, 256)
            elif ci == 2:
                wb, wv = self.loadw("b_w_in", j, B_OFFS[1], 8, 128)
            cs_ = slice((ci % 2) * 128, (ci % 2 + 1) * 128) if ci < 2 else slice(0, 128)
            ps = c.bank()
            for k in range(8):
                c.mm(ps[:, 0:TB], wv[:, k, cs_], xn[:, k, :], k == 0, k == 7, R=[wb.r, xn.rs[k]], W=[ps.r])
            f = self.fpool()
            qlf.append(f)
            c.actf(f[:, 0:TB], ps[:, 0:TB], AF.Copy, R=[], W=[f.r, ps.r])
            sq = self.hpool()
            c.actf(sq[:, 0:TB], f[:, 0:TB], AF.Square, R=[f.r], W=[sq.r])
            c.mm(pss[:, 0:TB], self.ones[:, :], sq[:, 0:TB], ci == 0, ci == 2, R=[sq.r, self.ones.r], W=[pss.r])
        rs = self.rstd_from(pss[:, 0:TB], pss.r, 128, TB, 1.0 / 384)
        for ci in range(3):
            c.stt(self.qln[:, ci, :], qlf[ci][:, 0:TB], self.vecs[:, V_QLATG + 3 * j + ci:V_QLATG + 3 * j + ci + 1],
                  rs[:, 0:TB], ALU.mult, ALU.mult, R=[qlf[ci].r, rs.r, self.vecs.r], W=[self.qln.r])
        q = [self.g_qhead(j, h) for h in range(8)]
        z = [self.g_zgroup(j, g) for g in range(4)]
        m = [("m", self.g_memgroup("b_w_in", j, B_OFFS[6 + g2], B_OFFS[8 + g2], g2, L)) for g2 in range(2)]
        qt = [("qa" if h % 2 == 0 else "qb", q[h]) for h in range(8)]
        rr([qt[0], qt[1], ("z", z[0]), qt[2], qt[3], ("z", z[1]), qt[4], qt[5], qt[6], qt[7]], 3, stagger=7)
        rr([m[0], z[2], m[1], z[3]], 2)
        c.nrot = 4
        rr([self.g_attn_head(b, h, h % 2) for h in range(8)], 2, stagger=(NT * (b + 1) + 4) // 2)
        self.wout("b_w_out", j, b)


@_P(Prog)
def cast_next(self, p):
    i = self.phases.index(p)
    if i + 1 < len(self.phases):
        self.precast(self._wl(self.phases[i + 1]))


@_P(Prog)
def build(self):
    c = self.c
    self.decl()
    c.init_psum(8)
    self.alloc_common()
    self.load_consts()
    def wlist(p):
        if p[0] == "A":
            l = int(p[1])
            return [("mem_w_kv", l), ("a_w_in", l), ("a_w_out", l)]
        if p == "KV":
            return [("w_kva", None), ("w_kvb", None)]
        j = int(p[1])
        return [("mem_w_kv", 2 + j), ("b_w_in", j), ("b_w_qup", j), ("b_w_out", j)]

    self.precast(wlist(self.phases[0]))
    self._wl = wlist
    for s in range(self.nseq):
        self.cur_seq = s
        aph = [p for p in self.phases if p[0] == "A"]
        bph = [p for p in self.phases if p[0] != "A"]
        if aph:
            with ExitStack() as es:
                self.allocA(es)
                self.load_seq(s)
                for p in aph:
                    self.cast_next(p)
                    self.layerA(s, int(p[1]))
                c.barrier()
        if bph:
            with ExitStack() as es:
                self.allocB(es)
                c.nrot = 4
                self.setupB()
                if not aph:
                    self.load_seq(s)
                for p in bph:
                    self.cast_next(p)
                    if p == "KV":
                        self.sharedkv(s)
                    else:
                        self.layerB(s, int(p[1]))
                c.barrier()
                c.nrot = 8
        self.store_seq(s)
    c.sp.obj.wait_ge(self.ochan.sem, self.ochan.val)
    c.barrier()
    c.es.close()
    return self.nc


def prep_shared(inp):
    f = lambda a: np.ascontiguousarray(np.asarray(a, dtype=np.float32))
    out = {}
    out["consts"] = make_consts()
    out["a_w_in"] = np.stack([pack_w(f(inp["a_w_in"][l]), A_GROUPS) for l in range(2)])
    out["a_w_out"] = np.stack([pack_w(f(inp["a_w_out"][l]), OUT_GROUPS) for l in range(2)])
    out["b_w_in"] = np.stack([pack_w(f(inp["b_w_in"][l]), B_GROUPS) for l in range(2)])
    out["b_w_qup"] = np.stack([pack_w(f(inp["b_w_q_up"][l]), QUP_GROUPS) for l in range(2)])
    out["b_w_out"] = np.stack([pack_w(f(inp["b_w_out"][l]), OUT_GROUPS) for l in range(2)])
    out["w_kva"] = pack_w(f(inp["w_kv_a"]), KVA_GROUPS)
    wkvb = f(inp["w_kv_b"]).reshape(256, 8, 2, 128)
    wkvb = np.concatenate([wkvb[:, :, 0, :].reshape(256, 1024), wkvb[:, :, 1, :].reshape(256, 1024)], axis=1)
    out["w_kvb"] = pack_w(wkvb, KVB_GROUPS)
    out["mem_w_kv"] = np.stack([pack_w(f(inp["mem_w_kv"][l]), MKV_GROUPS) for l in range(4)])
    v = np.zeros((128, NVEC), np.float32)

    def col8(x):
        return f(x).reshape(8, 128).T

    for l in range(2):
        v[:, V_ANORM + 8 * l:V_ANORM + 8 * l + 8] = col8(inp["a_norm"][l])
        v[:, V_BNORM + 8 * l:V_BNORM + 8 * l + 8] = col8(inp["b_norm"][l])
        cw = f(inp["a_conv_w"][l])
        v[:, V_CONVW + 64 * l:V_CONVW + 64 * l + 64] = cw.reshape(4, 16, 128).transpose(2, 1, 0).reshape(128, 64)
        v[:, V_CONVB + 16 * l:V_CONVB + 16 * l + 16] = f(inp["a_conv_b"][l]).reshape(16, 128).T
        v[:, V_QLATG + 3 * l:V_QLATG + 3 * l + 3] = f(inp["b_q_lat_norm"][l]).reshape(3, 128).T
        v[:, V_QG_NOPE + l] = f(inp["b_q_gain"][l])[0:128]
        v[0:64, V_QG_ROPE + l] = f(inp["b_q_gain"][l])[128:192]
        v[0:4, V_IB + l] = f(inp["a_ig_bias"][l])
        v[0:4, V_NFB + l] = f(inp["a_fg_bias"][l])
    v[:, V_KVNORM:V_KVNORM + 8] = col8(inp["kv_norm"])
    v[:, V_MEMNORM:V_MEMNORM + 8] = col8(inp["mem_norm"])
    for L in range(4):
        v[:, V_MQG + L] = f(inp["mem_q_gain"][L])
        v[:, V_MKG + L] = f(inp["mem_k_gain"][L])
    v[:, V_KVLATG:V_KVLATG + 2] = f(inp["kv_lat_norm"]).reshape(2, 128).T
    v[:, V_KG_NOPE] = f(inp["k_gain"])[0:128]
    v[0:64, V_KG_ROPE] = f(inp["k_gain"])[128:192]
    out["vecs"] = v
    out["rows"] = f(inp["a_h_norm"]).reshape(1, 2048)
    return out


def prep_core(inp, b0, nseq):
    x = np.asarray(inp["x"][b0:b0 + nseq], dtype=np.float32)
    mem = np.asarray(inp["mem"][b0:b0 + nseq], dtype=np.float32)
    pos = np.asarray(inp["positions"][b0:b0 + nseq]).astype(np.int32)
    return {
        "xT": np.ascontiguousarray(x.transpose(0, 2, 1)),
        "memT": np.ascontiguousarray(mem.transpose(0, 2, 1)),
        "pos": np.ascontiguousarray(np.broadcast_to(pos[:, None, :], (nseq, 64, pos.shape[1]))),
    }


_CACHE = {}


def run(inp, phases, ncores, nseq, T, debug=False):
    key = (tuple(phases), nseq, T, debug)
    if key not in _CACHE:
        p = Prog(T, nseq, phases)
        p.debug = debug
        _CACHE[key] = p.build()
    nc = _CACHE[key]
    shared = prep_shared(inp)
    in_maps = []
    for ci in range(ncores):
        m = dict(shared)
        m.update(prep_core(inp, ci * nseq, nseq))
        in_maps.append(m)
    res = run_bass_kernel_spmd(nc, in_maps, core_ids=list(range(ncores)))
    if debug:
        return res.results
    outs = [np.asarray(r["outT"]).transpose(0, 2, 1) for r in res.results]
    return np.ascontiguousarray(np.concatenate(outs, axis=0)).astype(np.float32)


def kernel(**inputs):
    return run(inputs, ["A0", "A1", "KV", "B0", "B1"], 8, 2, 2048)
```

```python
import bisect
import math
from contextlib import ExitStack

import numpy as np
import concourse.bass as bass
import concourse.mybir as mybir
from concourse.bass_utils import run_bass_kernel_spmd

F32 = mybir.dt.float32
BF16 = mybir.dt.bfloat16
I32 = mybir.dt.int32
AF = mybir.ActivationFunctionType
ALU = mybir.AluOpType
AX = mybir.AxisListType

D = 1024
NMEM = 256
TB = 256
EPS = 1e-6
A_IN = 6152
B_IN = 2432


class Res:
    __slots__ = ("name", "w", "r", "dead")

    def __init__(self, name):
        self.name = name
        self.w = None
        self.r = {}
        self.dead = False


def realloc(tl):
    old = tl.r
    new = Res(old.name)
    new.w = old.w
    new.r = old.r
    old.dead = True
    n = Tl(tl.t, old.name)
    n.r = new
    n.rs = [new]
    if hasattr(tl, "chan"):
        n.chan = tl.chan
    return n


class Chan:
    def __init__(self, sem):
        self.sem = sem
        self.val = 0
        self.name = "chan"

    def resolve(self, v):
        return v


class Eng:
    def __init__(self, name, obj, sem):
        self.name = name
        self.obj = obj
        self.sem = sem
        self.n = 0
        self.val = 0
        self.sig_idx = []
        self.sig_val = []
        self.last = None
        self.last_idx = 0
        self.seen = {}

    def resolve(self, idx):
        i = bisect.bisect_left(self.sig_idx, idx)
        if i < len(self.sig_idx):
            return self.sig_val[i]
        assert self.last_idx >= idx and self.last is not None
        self.last.then_inc(self.sem, 1)
        self.val += 1
        self.sig_idx.append(self.last_idx)
        self.sig_val.append(self.val)
        return self.val


class Tl:
    def __init__(self, t, name, nres=1):
        self.t = t
        self.r = Res(name)
        self.rs = [Res(f"{name}.{i}") for i in range(nres)] if nres > 1 else [self.r]

    def __getitem__(self, k):
        return self.t[k]


class Ctx:
    def __init__(self):
        self.nc = bass.Bass("TRN2", target_bir_lowering=False)
        self.es = ExitStack()
        nc = self.nc
        self.pe = Eng("pe", nc.tensor, self._sem("s_pe"))
        self.act = Eng("act", nc.scalar, self._sem("s_act"))
        self.dve = Eng("dve", nc.vector, self._sem("s_dve"))
        self.pool = Eng("pool", nc.gpsimd, self._sem("s_pool"))
        self.sp = Eng("sp", nc.sync, self._sem("s_sp"))
        self.engs = [self.pe, self.act, self.dve, self.pool, self.sp]
        self.chans = []
        self.nbank = 0
        self.banks = []
        self.bbanks = []
        self.nbb = 0
        self.ninst = 0
        self.nrot = 8

    def _sem(self, name):
        return self.es.enter_context(self.nc.semaphore(name))

    def chan(self):
        ch = Chan(self._sem(f"s_ch{len(self.chans)}"))
        self.chans.append(ch)
        return ch

    def sb(self, name, shape, dt, nres=1, es=None):
        self.nsb = getattr(self, "nsb", 0) + 1
        t = (es or self.es).enter_context(self.nc.sbuf_tensor(f"sb{self.nsb}_{name}", list(shape), dt))
        return Tl(t, name, nres)

    def init_psum(self, nf=8):
        for i in range(nf):
            t = self.es.enter_context(self.nc.psum_tensor(f"psf{i}", [128, 512], F32))
            self.banks.append(Tl(t, f"psf{i}"))

    def bank(self):
        i = self.nbank % self.nrot
        self.nbank += 1
        self.banks[i] = realloc(self.banks[i])
        return self.banks[i]

    def bbank(self):
        i = self.nbb % len(self.bbanks)
        self.nbb += 1
        self.bbanks[i] = realloc(self.bbanks[i])
        return self.bbanks[i]

    def issue(self, E, fn, R=(), W=(), chan=None):
        raw = {}
        war = {}

        def add(d, tok):
            o, x = tok
            if o not in d or d[o] < x:
                d[o] = x

        for r in R:
            assert not r.dead, f"stale pool handle {r.name}"
            if r.w is not None:
                add(raw, r.w)
        for w in W:
            assert not w.dead, f"stale pool handle {w.name}"
            if w.w is not None:
                add(raw, w.w)
            for o, x in w.r.items():
                add(war, (o, x))
        need = dict(raw)
        for o, x in war.items():
            if o not in need or need[o] < x:
                need[o] = x
        waits = []
        for o, x in need.items():
            if o is E and E is self.pe:
                continue
            v = o.resolve(x)
            if E.seen.get(o, 0) >= v:
                continue
            E.seen[o] = v
            waits.append((o.sem, v))
        if chan is not None:
            for sem, v in waits:
                E.obj.wait_ge(sem, v)
            inst = fn()
            chan.val += 16
            inst.then_inc(chan.sem, 16)
            tok = (chan, chan.val)
        else:
            for sem, v in waits[:-1]:
                E.obj.wait_ge(sem, v)
            inst = fn()
            if waits:
                inst._wait_ge(*waits[-1])
            E.n += 1
            E.last = inst
            E.last_idx = E.n
            tok = (E, E.n)
        self.ninst += 1
        for r in R:
            o, x = tok
            if o not in r.r or r.r[o] < x:
                r.r[o] = x
        for w in W:
            w.w = tok
            w.r = {}
        return inst

    def barrier(self):
        toks = []
        for F in self.engs:
            if F.last is not None:
                toks.append((F, F.sem, F.resolve(F.last_idx)))
        for ch in self.chans:
            if ch.val:
                toks.append((ch, ch.sem, ch.val))
        for E in self.engs:
            for o, sem, v in toks:
                if o is E:
                    continue
                if E.seen.get(o, 0) >= v:
                    continue
                E.seen[o] = v
                E.obj.wait_ge(sem, v)

    def mm(self, out, lhsT, rhs, start, stop, R, W):
        return self.issue(self.pe, lambda: self.nc.tensor.matmul(out, lhsT, rhs, start=start, stop=stop), R, W)

    def tr(self, out, in_, ident, R, W):
        return self.issue(self.pe, lambda: self.nc.tensor.transpose(out, in_, ident), R, W)

    def actf(self, out, in_, func, R, W, bias=None, scale=1.0, accum_out=None):
        kw = {}
        if bias is not None:
            kw["bias"] = bias
        if accum_out is not None:
            kw["accum_out"] = accum_out
        return self.issue(self.act, lambda: self.nc.scalar.activation(out=out, in_=in_, func=func, scale=scale, **kw), R, W)

    def tt(self, out, a, b, op, R, W, eng=None):
        E = eng or self.dve
        return self.issue(E, lambda: E.obj.tensor_tensor(out=out, in0=a, in1=b, op=op), R, W)

    def ts(self, out, a, s1, s2, op0, op1, R, W, eng=None):
        E = eng or self.dve
        if op1 is None:
            return self.issue(E, lambda: E.obj.tensor_scalar(out=out, in0=a, scalar1=s1, scalar2=None, op0=op0), R, W)
        return self.issue(E, lambda: E.obj.tensor_scalar(out=out, in0=a, scalar1=s1, scalar2=s2, op0=op0, op1=op1), R, W)

    def stt(self, out, in0, scalar, in1, op0, op1, R, W, eng=None):
        E = eng or self.dve
        return self.issue(E, lambda: E.obj.scalar_tensor_tensor(out=out, in0=in0, scalar=scalar, in1=in1, op0=op0, op1=op1), R, W)

    def cp(self, out, in_, R, W, eng=None):
        E = eng or self.dve
        return self.issue(E, lambda: E.obj.tensor_copy(out=out, in_=in_), R, W)

    def recip(self, out, in_, R, W):
        return self.issue(self.dve, lambda: self.nc.vector.reciprocal(out=out, in_=in_), R, W)

    def scan(self, out, d0, d1, init, op0, op1, R, W):
        return self.issue(self.dve, lambda: self.nc.vector.tensor_tensor_scan(out=out, data0=d0, data1=d1, initial=init, op0=op0, op1=op1), R, W)

    def memset(self, ap, v, W, eng=None):
        E = eng or self.dve
        return self.issue(E, lambda: E.obj.memset(ap, v), (), W)

    def dma(self, q, out, in_, R, W, chan):
        return self.issue(q, lambda: q.obj.dma_start(out=out, in_=in_), R, W, chan=chan)


def pack_w(W, groups):
    K = W.shape[0]
    kc = K // 128
    outs = []
    for (c0, wd) in groups:
        blk = W[:, c0:c0 + wd].reshape(kc, 128, wd).transpose(1, 0, 2).reshape(128, kc * wd)
        outs.append(blk)
    return np.ascontiguousarray(np.concatenate(outs, axis=1))


def group_offsets(kc, groups):
    offs = []
    o = 0
    for (_, wd) in groups:
        offs.append(o)
        o += kc * wd
    return offs, o


A_GROUPS = ([(i * 256, 256) for i in range(8)] +
            [(2048 + i * 256, 256) for i in range(4)] +
            [(3072 + i * 256, 256) for i in range(4)] +
            [(4096 + i * 256, 256) for i in range(4)] +
            [(5128 + i * 256, 256) for i in range(2)] +
            [(5640 + i * 256, 256) for i in range(2)] +
            [(5120, 8)])
A_OFFS, A_TOT = group_offsets(8, A_GROUPS)
OUT_GROUPS = [(i * 128, 128) for i in range(8)]
OUT_OFFS, OUT_TOT = group_offsets(12, OUT_GROUPS)
MKV_GROUPS = [(i * 256, 256) for i in range(4)]
MKV_OFFS, MKV_TOT = group_offsets(8, MKV_GROUPS)
B_GROUPS = ([(0, 256), (256, 128)] +
            [(384 + i * 256, 256) for i in range(4)] +
            [(1408 + i * 256, 256) for i in range(2)] +
            [(1920 + i * 256, 256) for i in range(2)])
B_OFFS, B_TOT = group_offsets(8, B_GROUPS)
QUP_GROUPS = [(h * 192, 192) for h in range(8)]
QUP_OFFS, QUP_TOT = group_offsets(3, QUP_GROUPS)
KVA_GROUPS = [(0, 256), (256, 64)]
KVA_OFFS, KVA_TOT = group_offsets(8, KVA_GROUPS)
KVB_GROUPS = [(0, 256), (256, 256), (512, 256), (768, 256), (1024, 256), (1280, 256), (1536, 256), (1792, 256)]
KVB_OFFS, KVB_TOT = group_offsets(2, KVB_GROUPS)

NCONST = 128 * 3 + 256 + 64 + 8
C_ID, C_MA, C_MB, C_ONE, C_RM, C_MISC = 0, 128, 256, 384, 640, 704


def make_consts():
    c = np.zeros((128, NCONST), np.float32)
    c[:, C_ID:C_ID + 128] = np.eye(128, dtype=np.float32)
    s = np.arange(128)[:, None]
    t = np.arange(128)[None, :]
    c[:, C_MA:C_MA + 128] = (s <= t).astype(np.float32)
    c[:, C_MB:C_MB + 128] = ((s // 64) <= (t // 64)).astype(np.float32)
    c[:, C_ONE:C_ONE + 256] = 1.0
    rm = np.zeros((64, 64), np.float32)
    for m in range(32):
        rm[m + 32, m] = -1.0
    for m in range(32, 64):
        rm[m - 32, m] = 1.0
    c[0:64, C_RM:C_RM + 64] = rm
    inv = (10000.0 ** (-np.arange(0, 64, 2, dtype=np.float32) / 64)).astype(np.float32)
    c[0:32, C_MISC] = inv
    c[32:64, C_MISC] = inv
    return c


class Prog:
    def __init__(self, T, nseq, phases):
        self.T = T
        self.nseq = nseq
        self.phases = phases
        self.c = Ctx()
        self.nc = self.c.nc
        self.nb = T // TB
        self.debug = False
        self.dbgd = {}

    def decl(self):
        nc, T, ns = self.nc, self.T, self.nseq
        d = {}

        def inp(name, shape, dt=F32):
            d[name] = nc.dram_tensor(name, list(shape), dt, kind="ExternalInput").ap()

        inp("xT", [ns, D, T])
        inp("memT", [ns, D, NMEM])
        inp("pos", [ns, 64, T], I32)
        inp("consts", [128, NCONST])
        inp("a_w_in", [2, 128, A_TOT])
        inp("a_w_out", [2, 128, OUT_TOT])
        inp("b_w_in", [2, 128, B_TOT])
        inp("b_w_qup", [2, 128, QUP_TOT])
        inp("b_w_out", [2, 128, OUT_TOT])
        inp("w_kva", [128, KVA_TOT])
        inp("w_kvb", [128, KVB_TOT])
        inp("mem_w_kv", [4, 128, MKV_TOT])
        inp("vecs", [128, NVEC])
        inp("rows", [1, NROW])
        d["outT"] = nc.dram_tensor("outT", [ns, D, T], F32, kind="ExternalOutput").ap()
        self.d = d
        self.dres = {k: Res("dram_" + k) for k in d}
        self.wbf = {}
        self.wbf_res = {}
        self.wbf_chan = {}
        for name in ("a_w_in", "a_w_out", "mem_w_kv", "w_kva", "w_kvb", "b_w_in", "b_w_qup", "b_w_out"):
            shp = list(d[name].shape)
            self.wbf[name] = nc.dram_tensor(name + "_bf", shp, BF16, kind="Internal").ap()

    def precast(self, order):
        c = self.c
        CH = 8192
        for (name, layer) in order:
            key = (name, layer)
            if key in self.wbf_res:
                continue
            self.wbf_res[key] = Res(f"wbf_{name}_{layer}")
            ch = c.chan()
            self.wbf_chan[key] = ch
            src = self.d[name]
            dst = self.wbf[name]
            tot = src.shape[-1]
            for o in range(0, tot, CH):
                n = min(CH, tot - o)
                if layer is None:
                    sa, da = src[:, o:o + n], dst[:, o:o + n]
                else:
                    sa, da = src[layer, :, o:o + n], dst[layer, :, o:o + n]
                c.dma(c.pool, da, sa, R=[self.dres[name]], W=[self.wbf_res[key]], chan=ch)


V_ANORM = 0
V_BNORM = 16
V_KVNORM = 32
V_MEMNORM = 40
V_CONVW = 48
V_CONVB = 176
V_MQG = 208
V_MKG = 212
V_QLATG = 216
V_KVLATG = 222
V_QG_NOPE = 224
V_QG_ROPE = 226
V_KG_NOPE = 228
V_KG_ROPE = 229
V_IB = 230
V_NFB = 232
NVEC = 240
NROW = 2048


def _P(cls):
    def deco(f):
        setattr(cls, f.__name__, f)
        return f
    return deco


NT = TB // 128


def rr(gens, width, stagger=0):
    pending = [g if isinstance(g, tuple) else (None, g) for g in gens]
    active = []
    rnd = 0
    while True:
        w = width if not stagger else min(width, 1 + rnd // stagger)
        i = 0
        while len(active) < w and i < len(pending):
            tag = pending[i][0]
            if tag is not None and any(t == tag for t, _ in active):
                i += 1
                continue
            active.append(pending.pop(i))
        if not active:
            assert not pending
            return
        for item in list(active):
            try:
                next(item[1])
            except StopIteration:
                active.remove(item)
        rnd += 1


def rr_gen(gens):
    active = list(gens)
    while active:
        for g in list(active):
            try:
                next(g)
            except StopIteration:
                active.remove(g)
        yield


@_P(Prog)
def alloc_common(self):
    c, T = self.c, self.T
    self.xT = c.sb("xT", [128, 8, T], F32, nres=self.nb * 8)
    self.xT.chan = c.chan()
    self.ochan = c.chan()
    self.cst = c.sb("cst", [128, NCONST], F32)
    self.cst.chan = c.chan()
    self.vecs = c.sb("vecs", [128, NVEC], F32)
    self.vecs.chan = c.chan()
    self.negfb = c.sb("negfb", [128, 2], F32)
    self.ident = c.sb("ident", [128, 128], BF16)
    self.ones = c.sb("ones", [128, 128], BF16)
    self.maskB = c.sb("maskB", [128, 128], BF16)
    self.lnsc = c.sb("lnsc", [128, 1], F32)
    self.mask16 = c.sb("mask16", [128, 128], F32)
    self.rmT = c.sb("rmT", [64, 64], BF16)
    self.xn = c.sb("xn", [128, 8, TB], BF16, nres=8)
    self.ymix = Tl(self.xn.t, "ymix")
    self.ymix.r = self.xn.rs[0]
    self.ymix.rs = self.xn.rs
    self.ymem = c.sb("ymem", [128, 4, TB], BF16, nres=4)
    self.mkT = c.sb("mkT", [128, 4, NMEM], BF16)
    self.mv = c.sb("mv", [128, 2, 512], BF16)
    self.wchans = [c.chan() for _ in range(9)]
    self.fchans = [c.chan() for _ in range(2)]


@_P(Prog)
def alloc_pools(self, es, nw, nfp, nhp, nsp):
    c = self.c
    self.wbufs = []
    for i in range(nw):
        w = c.sb(f"wbuf{i}", [128, 2048], BF16, es=es)
        w.chan = self.wchans[i]
        self.wbufs.append(w)
    self.nw = 0
    self.fp = [c.sb(f"fp{i}", [128, 260], F32, es=es) for i in range(nfp)]
    self.nfp = 0
    self.hp = [c.sb(f"hp{i}", [128, 256], BF16, es=es) for i in range(nhp)]
    self.nhp = 0
    self.spl = [c.sb(f"sp{i}", [128, 16], F32, es=es) for i in range(nsp)]
    self.nsp = 0


@_P(Prog)
def dbg(self, name, ap, R, shape, dt=F32):
    if not getattr(self, "debug", False) or name in self.dbgd:
        return
    c = self.c
    t = self.nc.dram_tensor("dbg_" + name, list(shape), dt, kind="ExternalOutput").ap()
    self.dbgd[name] = t
    c.dma(c.sp, t, ap, R=R, W=[Res("dbg")], chan=self.ochan)


@_P(Prog)
def fpool(self):
    i = self.nfp % len(self.fp)
    self.nfp += 1
    self.fp[i] = realloc(self.fp[i])
    return self.fp[i]


@_P(Prog)
def hpool(self):
    i = self.nhp % len(self.hp)
    self.nhp += 1
    self.hp[i] = realloc(self.hp[i])
    return self.hp[i]


@_P(Prog)
def spool(self):
    i = self.nsp % len(self.spl)
    self.nsp += 1
    self.spl[i] = realloc(self.spl[i])
    return self.spl[i]


@_P(Prog)
def load_consts(self):
    c, d = self.c, self.d
    c.dma(c.sp, self.cst[:, :], d["consts"][:, :], R=[self.dres["consts"]], W=[self.cst.r], chan=self.cst.chan)
    c.dma(c.sp, self.vecs[:, :], d["vecs"][:, :], R=[self.dres["vecs"]], W=[self.vecs.r], chan=self.vecs.chan)
    c.cp(self.ident[:, :], self.cst[:, C_ID:C_ID + 128], R=[self.cst.r], W=[self.ident.r])
    c.cp(self.ones[:, :], self.cst[:, C_ONE:C_ONE + 128], R=[self.cst.r], W=[self.ones.r])
    c.cp(self.maskB[:, :], self.cst[:, C_MB:C_MB + 128], R=[self.cst.r], W=[self.maskB.r])
    c.memset(self.lnsc[:, :], math.log(192 ** -0.5), W=[self.lnsc.r])
    c.ts(self.mask16[:, :], self.cst[:, C_MA:C_MA + 128], 0.0625, None, ALU.mult, None, R=[self.cst.r], W=[self.mask16.r])
    c.cp(self.rmT[:, :], self.cst[0:64, C_RM:C_RM + 64], R=[self.cst.r], W=[self.rmT.r])
    c.ts(self.negfb[:, :], self.vecs[:, V_NFB:V_NFB + 2], -1.0, None, ALU.mult, None, R=[self.vecs.r], W=[self.negfb.r])


@_P(Prog)
def load_seq(self, s):
    c, d = self.c, self.d
    for k in range(8):
        c.dma(c.sp, self.xT[:, k, :], d["xT"][s, k * 128:(k + 1) * 128, :], R=[self.dres["xT"]],
              W=self.xT.rs, chan=self.xT.chan)


@_P(Prog)
def memnorm(self, s):
    c, d = self.c, self.d
    ps = c.bank()
    mts = [self.fpool(), self.fpool()]
    for i in range(2):
        mts[i].chan = self.fchans[i]
    for k in range(8):
        mt = mts[k % 2]
        c.dma(c.sp, mt[:, 0:NMEM], d["memT"][s, k * 128:(k + 1) * 128, :], R=[self.dres["memT"]], W=[mt.r], chan=mt.chan)
        sq = self.hpool()
        c.actf(sq[:, :], mt[:, 0:NMEM], AF.Square, R=[mt.r], W=[sq.r])
        c.mm(ps[:, 0:NMEM], self.ones[:, :], sq[:, :], k == 0, k == 7, R=[sq.r, self.ones.r], W=[ps.r])
    rstd = self.rstd_from(ps[:, 0:NMEM], ps.r, 128, NMEM, 1.0 / D)
    for k in range(8):
        mt = mts[k % 2]
        c.dma(c.sp, mt[:, 0:NMEM], d["memT"][s, k * 128:(k + 1) * 128, :], R=[self.dres["memT"]], W=[mt.r], chan=mt.chan)
        c.stt(self.memn[:, k, :], mt[:, 0:NMEM], self.vecs[:, V_MEMNORM + k:V_MEMNORM + k + 1],
              rstd[:, 0:NMEM], ALU.mult, ALU.mult, R=[mt.r, rstd.r, self.vecs.r], W=self.memn.rs)


@_P(Prog)
def store_seq(self, s):
    c, d = self.c, self.d
    for k in range(8):
        c.dma(c.sp, d["outT"][s, k * 128:(k + 1) * 128, :], self.xT[:, k, :], R=self.xT.rs,
              W=[self.dres["outT"]], chan=self.ochan)


@_P(Prog)
def rstd_from(self, ps_ap, ps_res, P, n, inv_dim):
    c = self.c
    r = self.fpool()
    c.ts(r[0:P, 0:n], ps_ap, inv_dim, EPS, ALU.mult, ALU.add, R=[], W=[r.r, ps_res])
    c.actf(r[0:P, 0:n], r[0:P, 0:n], AF.Ln, R=[], W=[r.r])
    c.actf(r[0:P, 0:n], r[0:P, 0:n], AF.Exp, R=[], W=[r.r], scale=-0.5)
    return r


@_P(Prog)
def loadw(self, wname, layer, goff, kc, wd):
    c = self.c
    i = self.nw % len(self.wbufs)
    self.nw += 1
    self.wbufs[i] = realloc(self.wbufs[i])
    wb = self.wbufs[i]
    src = self.wbf[wname]
    n = kc * wd
    src_ap = src[layer, :, goff:goff + n] if layer is not None else src[:, goff:goff + n]
    c.dma(c.sp, wb[:, 0:n], src_ap, R=[self.wbf_res[(wname, layer)]], W=[wb.r], chan=wb.chan)
    return wb, wb.t[:, 0:n].rearrange("p (k c) -> p k c", k=kc)


@_P(Prog)
def xnorm(self, b, gcol):
    c = self.c
    blk = slice(b * TB, (b + 1) * TB)
    ps = c.bank()
    for k in range(8):
        sq = self.hpool()
        c.actf(sq[:, :], self.xT[:, k, blk], AF.Square, R=[self.xT.rs[b * 8 + k]], W=[sq.r])
        c.mm(ps[:, 0:TB], self.ones[:, :], sq[:, :], k == 0, k == 7, R=[sq.r, self.ones.r], W=[ps.r])
    rstd = self.rstd_from(ps[:, 0:TB], ps.r, 128, TB, 1.0 / D)
    for k in range(8):
        c.stt(self.xn[:, k, :], self.xT[:, k, blk], self.vecs[:, gcol + k:gcol + k + 1], rstd[:, 0:TB],
              ALU.mult, ALU.mult, R=[self.xT.rs[b * 8 + k], rstd.r, self.vecs.r], W=[self.xn.rs[k]])


@_P(Prog)
def memkv(self, L):
    c = self.c
    for g in range(4):
        wb, wv = self.loadw("mem_w_kv", L, MKV_OFFS[g], 8, 256)
        if g < 2:
            for half in range(2):
                h = 2 * g + half
                ps = c.bank()
                for k in range(8):
                    c.mm(ps[:, 0:NMEM], wv[:, k, half * 128:(half + 1) * 128], self.memn[:, k, :], k == 0, k == 7,
                         R=[wb.r] + self.memn.rs, W=[ps.r])
                kf = self.fpool()
                c.actf(kf[:, 0:NMEM], ps[:, 0:NMEM], AF.Copy, R=[], W=[kf.r, ps.r])
                sq = self.hpool()
                c.actf(sq[:, :], kf[:, 0:NMEM], AF.Square, R=[kf.r], W=[sq.r])
                pss = c.bank()
                c.mm(pss[:, 0:NMEM], self.ones[:, :], sq[:, :], True, True, R=[sq.r, self.ones.r], W=[pss.r])
                rs = self.rstd_from(pss[:, 0:NMEM], pss.r, 128, NMEM, 1.0 / 128)
                c.stt(self.mkT[:, h, :], kf[:, 0:NMEM], self.vecs[:, V_MKG + L:V_MKG + L + 1], rs[:, 0:NMEM],
                      ALU.mult, ALU.mult, R=[kf.r, rs.r, self.vecs.r], W=[self.mkT.r])
        else:
            for mc in range(2):
                ps = c.bank()
                for k in range(8):
                    c.mm(ps[:, 0:256], self.memn[:, k, mc * 128:(mc + 1) * 128], wv[:, k, :], k == 0, k == 7,
                         R=[wb.r] + self.memn.rs, W=[ps.r])
                c.actf(self.mv[:, mc, (g - 2) * 256:(g - 1) * 256], ps[:, 0:256], AF.Copy, R=[], W=[self.mv.r, ps.r])


@_P(Prog)
def g_memattn(self, wq, wbq, wz, wbz, half, h, L):
    c = self.c
    cs = slice(half * 128, (half + 1) * 128)
    psq = c.bank()
    for k in range(8):
        c.mm(psq[:, 0:TB], wq[:, k, cs], self.xn[:, k, :], k == 0, k == 7, R=[wbq.r, self.xn.rs[k]], W=[psq.r])
    yield
    mqf = self.fpool()
    c.actf(mqf[:, 0:TB], psq[:, 0:TB], AF.Copy, R=[], W=[mqf.r, psq.r])
    yield
    sq = self.hpool()
    c.actf(sq[:, 0:TB], mqf[:, 0:TB], AF.Square, R=[mqf.r], W=[sq.r])
    psz = c.bank()
    for k in range(8):
        c.mm(psz[:, 0:TB], wz[:, k, cs], self.xn[:, k, :], k == 0, k == 7, R=[wbz.r, self.xn.rs[k]], W=[psz.r])
    yield
    pss = c.bank()
    c.mm(pss[:, 0:TB], self.ones[:, :], sq[:, 0:TB], True, True, R=[sq.r, self.ones.r], W=[pss.r])
    sz = self.fpool()
    c.actf(sz[:, 0:TB], psz[:, 0:TB], AF.Sigmoid, R=[], W=[sz.r, psz.r])
    yield
    g1 = self.fpool()
    c.tt(g1[:, 0:TB], psz[:, 0:TB], sz[:, 0:TB], ALU.mult, R=[sz.r], W=[g1.r, psz.r])
    yield
    rs = self.fpool()
    c.ts(rs[:, 0:TB], pss[:, 0:TB], 1.0 / 128, EPS, ALU.mult, ALU.add, R=[], W=[rs.r, pss.r])
    yield
    c.actf(rs[:, 0:TB], rs[:, 0:TB], AF.Ln, R=[], W=[rs.r])
    yield
    c.actf(rs[:, 0:TB], rs[:, 0:TB], AF.Exp, R=[], W=[rs.r], scale=-0.5)
    yield
    qn = self.hpool()
    c.stt(qn[:, 0:TB], mqf[:, 0:TB], self.vecs[:, V_MQG + L:V_MQG + L + 1], rs[:, 0:TB], ALU.mult, ALU.mult,
          R=[mqf.r, rs.r, self.vecs.r], W=[qn.r])
    yield
    pT = []
    for mc in range(2):
        pssc = c.bank()
        c.mm(pssc[:, 0:TB], self.mkT[:, h, mc * 128:(mc + 1) * 128], qn[:, 0:TB], True, True,
             R=[self.mkT.r, qn.r], W=[pssc.r])
        p = self.hpool()
        c.actf(p[:, 0:TB], pssc[:, 0:TB], AF.Exp, R=[], W=[p.r, pssc.r], scale=128 ** -0.5)
        pT.append(p)
        yield
    psy = c.bank()
    for mc in range(2):
        c.mm(psy[:, 0:TB], self.mv[:, mc, h * 128:(h + 1) * 128], pT[mc][:, 0:TB], mc == 0, mc == 1,
             R=[self.mv.r, pT[mc].r], W=[psy.r])
    psd = c.bank()
    for mc in range(2):
        c.mm(psd[:, 0:TB], self.ones[:, :], pT[mc][:, 0:TB], mc == 0, mc == 1, R=[self.ones.r, pT[mc].r], W=[psd.r])
    yield
    rden = self.fpool()
    c.recip(rden[:, 0:TB], psd[:, 0:TB], R=[], W=[rden.r, psd.r])
    yield
    y1 = self.fpool()
    c.tt(y1[:, 0:TB], psy[:, 0:TB], rden[:, 0:TB], ALU.mult, R=[rden.r], W=[y1.r, psy.r])
    yield
    c.tt(self.ymem[:, h, :], y1[:, 0:TB], g1[:, 0:TB], ALU.mult, R=[y1.r, g1.r], W=[self.ymem.rs[h]])


@_P(Prog)
def g_memgroup(self, wname, l, offq, offz, g2, L):
    wbq, wq = self.loadw(wname, l, offq, 8, 256)
    wbz, wz = self.loadw(wname, l, offz, 8, 256)
    yield
    yield from rr_gen([self.g_memattn(wq, wbq, wz, wbz, half, 2 * g2 + half, L) for half in range(2)])


@_P(Prog)
def g_wout(self, wname, l, b, jo):
    c = self.c
    blk = slice(b * TB, (b + 1) * TB)
    wb, wv = self.loadw(wname, l, OUT_OFFS[jo], 12, 128)
    yield
    ps = c.bank()
    for k in range(8):
        c.mm(ps[:, 0:TB], wv[:, k, :], self.ymix[:, k, :], k == 0, False, R=[wb.r, self.ymix.rs[k]], W=[ps.r])
    for k in range(4):
        c.mm(ps[:, 0:TB], wv[:, 8 + k, :], self.ymem[:, k, :], False, k == 3, R=[wb.r, self.ymem.rs[k]], W=[ps.r])
    yield
    c.tt(self.xT[:, jo, blk], self.xT[:, jo, blk], ps[:, 0:TB], ALU.add, R=[], W=[self.xT.rs[b * 8 + jo], ps.r])


@_P(Prog)
def wout(self, wname, l, b):
    rr([self.g_wout(wname, l, b, jo) for jo in range(8)], 3, stagger=1)


@_P(Prog)
def allocA(self, es):
    c = self.c
    self.alloc_pools(es, nw=9, nfp=26, nhp=16, nsp=16)
    self.qT = c.sb("qT", [128, 8, TB], BF16, nres=8, es=es)
    self.kT = c.sb("kT", [128, 8, TB], BF16, nres=8, es=es)
    self.ktok = c.sb("ktok", [128, NT, 1024], BF16, nres=NT, es=es)
    self.vaug = c.sb("vaug", [128, NT, 4, 258], BF16, nres=NT * 4, es=es)
    self.gate = c.sb("gate", [128, 8, TB], BF16, nres=8, es=es)
    self.memn = self.gate
    self.httok = [c.sb(f"httok{i}", [128, 1024], BF16, nres=4, es=es) for i in range(2)]
    self.hist = c.sb("hist", [128, 16, 3], F32, nres=16, es=es)
    self.C = c.sb("Cst", [128, 2, 4, 257], F32, nres=4, es=es)
    self.Cbf = [c.sb(f"Cbf{i}", [128, 2, 257], BF16, es=es) for i in range(4)]
    self.hg = c.sb("hg", [128, 1024], F32, es=es)
    self.hg.chan = c.chan()
    self.gt = c.sb("gt", [128, NT, 16], F32, es=es)
    self.carry = c.sb("carry", [4, 4], F32, es=es)
    self.wif = c.sb("wif", [128, 64], BF16, es=es)
    self.wif.chan = c.chan()
    self.dg = c.sb("dg", [4, NT, 4], F32, es=es)
    self.wat = c.sb("wat", [4, TB], F32, es=es)
    self.tht = c.sb("tht", [4, TB], F32, es=es)


@_P(Prog)
def g_qk_chunk(self, l, j, wb, wv, half):
    c = self.c
    xn = self.xn
    cw0 = V_CONVW + l * 64
    cb0 = V_CONVB + l * 16
    ps = c.bank()
    for k in range(8):
        c.mm(ps[:, 0:TB], wv[:, k, half * 128:(half + 1) * 128], xn[:, k, :], k == 0, k == 7, R=[wb.r, xn.rs[k]], W=[ps.r])
    yield
    cs = self.fpool()
    c.cp(cs[:, 0:3], self.hist[:, j, :], R=[self.hist.rs[j]], W=[cs.r])
    c.actf(cs[:, 3:3 + TB], ps[:, 0:TB], AF.Copy, R=[], W=[cs.r, ps.r])
    yield
    c.cp(self.hist[:, j, :], cs[:, TB:TB + 3], R=[cs.r], W=[self.hist.rs[j]])
    acc = self.fpool()
    wc = cw0 + j * 4
    c.ts(acc[:, 0:TB], cs[:, 3:3 + TB], self.vecs[:, wc + 3:wc + 4], self.vecs[:, cb0 + j:cb0 + j + 1],
         ALU.mult, ALU.add, R=[cs.r, self.vecs.r], W=[acc.r])
    yield
    for tap in (2, 1, 0):
        c.stt(acc[:, 0:TB], cs[:, tap:tap + TB], self.vecs[:, wc + tap:wc + tap + 1], acc[:, 0:TB],
              ALU.mult, ALU.add, R=[cs.r, self.vecs.r], W=[acc.r])
        yield
    sg = self.fpool()
    c.actf(sg[:, 0:TB], acc[:, 0:TB], AF.Sigmoid, R=[acc.r], W=[sg.r])
    yield
    if j < 8:
        c.tt(self.qT[:, j, :], acc[:, 0:TB], sg[:, 0:TB], ALU.mult, R=[acc.r, sg.r], W=[self.qT.rs[j]])
    else:
        c.tt(self.kT[:, j - 8, :], acc[:, 0:TB], sg[:, 0:TB], ALU.mult, R=[acc.r, sg.r], W=[self.kT.rs[j - 8]])


@_P(Prog)
def g_qk(self, l, g):
    wb, wv = self.loadw("a_w_in", l, A_OFFS[g], 8, 256)
    yield
    yield from rr_gen([self.g_qk_chunk(l, 2 * g + half, wb, wv, half) for half in range(2)])


@_P(Prog)
def g_oz_chunk(self, j, wo, wbo, wz, wbz, half):
    c, xn = self.c, self.xn
    cs_ = slice(half * 128, (half + 1) * 128)
    pso = c.bank()
    for k in range(8):
        c.mm(pso[:, 0:TB], wo[:, k, cs_], xn[:, k, :], k == 0, k == 7, R=[wbo.r, xn.rs[k]], W=[pso.r])
    yield
    so = self.fpool()
    c.actf(so[:, 0:TB], pso[:, 0:TB], AF.Sigmoid, R=[], W=[so.r, pso.r])
    psz = c.bank()
    for k in range(8):
        c.mm(psz[:, 0:TB], wz[:, k, cs_], xn[:, k, :], k == 0, k == 7, R=[wbz.r, xn.rs[k]], W=[psz.r])
    yield
    sz = self.fpool()
    c.actf(sz[:, 0:TB], psz[:, 0:TB], AF.Sigmoid, R=[], W=[sz.r, psz.r])
    yield
    g1 = self.fpool()
    c.tt(g1[:, 0:TB], psz[:, 0:TB], sz[:, 0:TB], ALU.mult, R=[sz.r], W=[g1.r, psz.r])
    yield
    c.tt(self.gate[:, j, :], g1[:, 0:TB], so[:, 0:TB], ALU.mult, R=[g1.r, so.r], W=[self.gate.rs[j]])


@_P(Prog)
def g_oz(self, l, g):
    wbo, wo = self.loadw("a_w_in", l, A_OFFS[12 + g], 8, 256)
    wbz, wz = self.loadw("a_w_in", l, A_OFFS[16 + g], 8, 256)
    yield
    yield from rr_gen([self.g_oz_chunk(2 * g + half, wo, wbo, wz, wbz, half) for half in range(2)])


@_P(Prog)
def g_v(self, l, hh):
    c, xn = self.c, self.xn
    wb, wv = self.loadw("a_w_in", l, A_OFFS[8 + hh], 8, 256)
    yield
    for tt_ in range(NT):
        tsl = slice(tt_ * 128, (tt_ + 1) * 128)
        ps = c.bank()
        for k in range(8):
            c.mm(ps[:, 0:256], xn[:, k, tsl], wv[:, k, :], k == 0, k == 7, R=[wb.r, xn.rs[k]], W=[ps.r])
        yield
        c.actf(self.vaug[:, tt_, hh, 0:256], ps[:, 0:256], AF.Copy, R=[self.gt.r],
               W=[self.vaug.rs[tt_ * 4 + hh], ps.r], scale=self.gt[:, tt_, hh:hh + 1])
        c.cp(self.vaug[:, tt_, hh, 256:257], self.gt[:, tt_, hh:hh + 1], R=[self.gt.r],
             W=[self.vaug.rs[tt_ * 4 + hh]])
        yield


@_P(Prog)
def g_ktrans(self, tt_, jj):
    c = self.c
    tsl = slice(tt_ * 128, (tt_ + 1) * 128)
    pb = c.bank()
    pbv = pb.t[:, :].bitcast(BF16)
    for j in range(4):
        c.tr(pbv[:, j * 128:(j + 1) * 128], self.kT[:, jj + j, tsl], self.ident[:, :],
             R=[self.kT.rs[jj + j], self.ident.r], W=[pb.r])
    yield
    c.actf(self.ktok[:, tt_, jj * 128:(jj + 4) * 128], pbv[:, 0:512], AF.Copy, R=[], W=[self.ktok.rs[tt_], pb.r])


@_P(Prog)
def g_gates(self, l):
    c, cst = self.c, self.cst
    xn = self.xn
    wif = self.wif.t[:, :].rearrange("p (k c) -> p k c", k=8)
    ones4 = cst[0:4, C_ONE:C_ONE + TB]
    I4 = cst[0:4, C_ID:C_ID + 4]
    Gi = c.bank()
    for k in range(8):
        c.mm(Gi[0:4, 0:TB], wif[:, k, 0:4], xn[:, k, :], k == 0, k == 7, R=[self.wif.r, xn.rs[k]], W=[Gi.r])
    Gf = c.bank()
    for k in range(8):
        c.mm(Gf[0:4, 0:TB], wif[:, k, 4:8], xn[:, k, :], k == 0, k == 7, R=[self.wif.r, xn.rs[k]], W=[Gf.r])
    yield
    it = self.fpool()
    c.actf(it[0:4, 0:TB], Gi[0:4, 0:TB], AF.Identity, R=[self.vecs.r], W=[it.r, Gi.r],
           bias=self.vecs[0:4, V_IB + l:V_IB + l + 1])
    et = self.fpool()
    c.actf(et[0:4, 0:TB], Gf[0:4, 0:TB], AF.Exp, R=[self.negfb.r], W=[et.r, Gf.r],
           bias=self.negfb[0:4, l:l + 1], scale=-1.0)
    yield
    lt = self.fpool()
    c.actf(lt[0:4, 0:TB], et[0:4, 0:TB], AF.Ln, R=[et.r], W=[lt.r], bias=1.0)
    yield
    Bn = self.fpool()
    c.scan(Bn[0:4, 0:TB], ones4, lt[0:4, 0:TB], self.carry[0:4, 0:1], ALU.mult, ALU.add,
           R=[cst.r, lt.r, self.carry.r], W=[Bn.r])
    yield
    At = self.fpool()
    c.tt(At[0:4, 0:TB], it[0:4, 0:TB], Bn[0:4, 0:TB], ALU.add, R=[it.r, Bn.r], W=[At.r])
    yield
    Mt = self.fpool()
    c.scan(Mt[0:4, 0:TB], ones4, At[0:4, 0:TB], self.carry[0:4, 1:2], ALU.mult, ALU.max,
           R=[cst.r, At.r, self.carry.r], W=[Mt.r])
    yield
    mp = self.spool()
    c.cp(mp[0:4, 0:1], self.carry[0:4, 1:2], R=[self.carry.r], W=[mp.r])
    for cc in range(1, NT):
        c.cp(mp[0:4, cc:cc + 1], Mt[0:4, cc * 128 - 1:cc * 128], R=[Mt.r], W=[mp.r])
    me = self.spool()
    for cc in range(NT):
        c.cp(me[0:4, cc:cc + 1], Mt[0:4, cc * 128 + 127:cc * 128 + 128], R=[Mt.r], W=[me.r])
    yield
    c.cp(self.carry[0:4, 0:1], Bn[0:4, TB - 1:TB], R=[Bn.r], W=[self.carry.r])
    c.cp(self.carry[0:4, 1:2], Mt[0:4, TB - 1:TB], R=[Mt.r], W=[self.carry.r])
    wat, tht = self.wat, self.tht
    for cc in range(NT):
        sl = slice(cc * 128, (cc + 1) * 128)
        c.ts(wat[0:4, sl], At[0:4, sl], me[0:4, cc:cc + 1], None, ALU.subtract, None, R=[At.r, me.r], W=[wat.r])
        c.ts(tht[0:4, sl], Bn[0:4, sl], me[0:4, cc:cc + 1], None, ALU.subtract, None, R=[Bn.r, me.r], W=[tht.r])
    yield
    c.actf(wat[0:4, 0:TB], wat[0:4, 0:TB], AF.Exp, R=[], W=[wat.r])
    c.actf(tht[0:4, 0:TB], tht[0:4, 0:TB], AF.Exp, R=[], W=[tht.r])
    gst = self.spool()
    c.tt(gst[0:4, 0:NT], mp[0:4, 0:NT], me[0:4, 0:NT], ALU.subtract, R=[mp.r, me.r], W=[gst.r])
    yield
    c.actf(gst[0:4, 0:NT], gst[0:4, 0:NT], AF.Exp, R=[], W=[gst.r])
    yield
    for cc in range(NT):
        c.ts(self.dg[0:4, cc, :], I4, gst[0:4, cc:cc + 1], None, ALU.mult, None, R=[cst.r, gst.r], W=[self.dg.r])


@_P(Prog)
def g_gtp(self):
    c, cst = self.c, self.cst
    I4 = cst[0:4, C_ID:C_ID + 4]
    gtp = c.bank()
    for cc in range(NT):
        sl = slice(cc * 128, (cc + 1) * 128)
        c.mm(gtp[:, cc * 12:cc * 12 + 4], self.wat[0:4, sl], I4, True, True, R=[self.wat.r, cst.r], W=[gtp.r])
        c.mm(gtp[:, cc * 12 + 4:cc * 12 + 8], self.tht[0:4, sl], I4, True, True, R=[self.tht.r, cst.r], W=[gtp.r])
        c.mm(gtp[:, cc * 12 + 8:cc * 12 + 12], cst[0:4, C_ONE:C_ONE + 128], self.dg[0:4, cc, :], True, True,
             R=[self.dg.r, cst.r], W=[gtp.r])
    yield
    c.cp(self.gt[:, :, 0:12], gtp[:, 0:NT * 12].rearrange("p (c t) -> p c t", t=12), R=[], W=[self.gt.r, gtp.r])
    c.ts(self.gt[:, :, 12:16], self.gt[:, :, 8:12], 0.0625, None, ALU.mult, None, R=[], W=[self.gt.r])


@_P(Prog)
def g_mlstm_head(self, tt_, h, ht):
    c = self.c
    tsl = slice(tt_ * 128, (tt_ + 1) * 128)
    sps = c.bank()
    for dd in range(2):
        c.mm(sps[:, 0:128], self.kT[:, 2 * h + dd, tsl], self.qT[:, 2 * h + dd, tsl], dd == 0, dd == 1,
             R=[self.kT.rs[2 * h + dd], self.qT.rs[2 * h + dd]], W=[sps.r])
    cb = self.Cbf[h]
    gsc = self.gt[:, tt_, 8 + h:9 + h]
    for dd in range(2):
        c.actf(cb[:, dd, :], self.C[:, dd, h, :], AF.Copy, R=[self.C.rs[h], self.gt.r], W=[cb.r],
               scale=self.gt[:, tt_, 12 + h:13 + h])
    yield
    p0 = self.hpool()
    c.tt(p0[:, 0:128], sps[:, 0:128], self.mask16[:, :], ALU.mult, R=[self.mask16.r], W=[p0.r, sps.r])
    va = self.vaug[:, tt_, h, 0:257]
    var = self.vaug.rs[tt_ * 4 + h]
    dps0 = c.bank()
    c.mm(dps0[:, 0:257], self.ktok[:, tt_, (2 * h) * 128:(2 * h + 1) * 128], va, True, True,
         R=[self.ktok.rs[tt_], var], W=[dps0.r])
    yield
    nps = c.bank()
    c.mm(nps[:, 0:257], p0[:, 0:128], va, True, False, R=[p0.r, var], W=[nps.r])
    for dd in range(2):
        c.mm(nps[:, 0:257], self.qT[:, 2 * h + dd, tsl], cb[:, dd, :], False, dd == 1,
             R=[self.qT.rs[2 * h + dd], cb.r], W=[nps.r])
    c.stt(self.C[:, 0, h, :], self.C[:, 0, h, :], gsc, dps0[:, 0:257], ALU.mult, ALU.add,
          R=[self.gt.r], W=[self.C.rs[h], dps0.r])
    yield
    dps1 = c.bank()
    c.mm(dps1[:, 0:257], self.ktok[:, tt_, (2 * h + 1) * 128:(2 * h + 2) * 128], va, True, True,
         R=[self.ktok.rs[tt_], var], W=[dps1.r])
    sm = self.spool()
    c.cp(sm[:, 4:5], nps[:, 256:257], R=[], W=[sm.r, nps.r])
    yield
    c.stt(sm[:, 5:6], sm[:, 4:5], -1.0, sm[:, 4:5], ALU.mult, ALU.max, R=[], W=[sm.r])
    yield
    c.stt(self.C[:, 1, h, :], self.C[:, 1, h, :], gsc, dps1[:, 0:257], ALU.mult, ALU.add,
          R=[self.gt.r], W=[self.C.rs[h], dps1.r])
    yield
    c.tt(sm[:, 0:1], sm[:, 5:6], self.gt[:, tt_, 4 + h:5 + h], ALU.max, R=[self.gt.r], W=[sm.r])
    yield
    c.recip(sm[:, 1:2], sm[:, 0:1], R=[], W=[sm.r])
    yield
    a = self.fpool()
    c.actf(a[:, 0:256], nps[:, 0:256], AF.Copy, R=[sm.r], W=[a.r, nps.r], scale=sm[:, 1:2])
    yield
    junk = self.hpool()
    c.actf(junk[:, 0:256], a[:, 0:256], AF.Square, R=[a.r], W=[junk.r, sm.r], accum_out=sm[:, 2:3])
    yield
    c.ts(sm[:, 3:4], sm[:, 2:3], 1.0 / 256, EPS, ALU.mult, ALU.add, R=[], W=[sm.r])
    yield
    c.actf(sm[:, 3:4], sm[:, 3:4], AF.Ln, R=[], W=[sm.r])
    yield
    c.actf(sm[:, 3:4], sm[:, 3:4], AF.Exp, R=[], W=[sm.r], scale=-0.5)
    yield
    c.stt(ht[:, h * 256:(h + 1) * 256], a[:, 0:256], sm[:, 3:4], self.hg[:, h * 256:(h + 1) * 256],
          ALU.mult, ALU.mult, R=[a.r, sm.r, self.hg.r], W=[ht.rs[h]])


@_P(Prog)
def g_httrans(self, tt_, jj, ht):
    c = self.c
    tsl = slice(tt_ * 128, (tt_ + 1) * 128)
    pb = c.bank()
    pbv = pb.t[:, :].bitcast(BF16)
    for j in range(4):
        c.tr(pbv[:, j * 128:(j + 1) * 128], ht[:, (jj + j) * 128:(jj + j + 1) * 128], self.ident[:, :],
             R=[ht.rs[(jj + j) // 2], self.ident.r], W=[pb.r])
    yield
    c.tt(self.ymix[:, jj:jj + 4, tsl], pbv[:, 0:512].rearrange("p (j t) -> p j t", t=128),
         self.gate[:, jj:jj + 4, tsl], ALU.mult, R=[self.gate.rs[jj + i] for i in range(4)],
         W=[self.ymix.rs[jj + i] for i in range(4)] + [pb.r])


@_P(Prog)
def layerA(self, s, l):
    c, d = self.c, self.d
    L = l
    c.dma(c.sp, self.hg[:, :], d["rows"][0:1, l * 1024:(l + 1) * 1024].to_broadcast([128, 1024]),
          R=[self.dres["rows"]], W=[self.hg.r], chan=self.hg.chan)
    c.dma(c.sp, self.wif[:, :], self.wbf["a_w_in"][l, :, A_OFFS[24]:A_OFFS[24] + 64],
          R=[self.wbf_res[("a_w_in", l)]], W=[self.wif.r], chan=self.wif.chan)
    c.memset(self.hist[:, :, :], 0.0, W=self.hist.rs)
    c.memset(self.C[:, :, :, :], 0.0, W=self.C.rs)
    c.memset(self.carry[:, :], 0.0, W=[self.carry.r])
    self.memnorm(s)
    self.memkv(L)
    for b in range(self.nb):
        if b == 0:
            self.xnorm(b, V_ANORM + l * 8)
        items = [self.g_gates(l)]
        for g in range(4):
            items += [self.g_qk(l, g), self.g_oz(l, g)]
        rr(items, 3, stagger=4)
        rr([self.g_gtp()], 1)
        mg = [self.g_memgroup("a_w_in", l, A_OFFS[20 + g2], A_OFFS[22 + g2], g2, L) for g2 in range(2)]
        items = [self.g_qk(l, 4), ("m", mg[0]), self.g_v(l, 0), self.g_qk(l, 5), self.g_v(l, 1), self.g_qk(l, 6),
                 ("m", mg[1]), self.g_v(l, 2), self.g_qk(l, 7), self.g_v(l, 3)]
        rr(items, 3, stagger=4)
        rr([self.g_ktrans(tt_, jj) for tt_ in range(NT) for jj in (0, 4)], 2)
        hoist = False
        if hoist:
            self.xnorm(b + 1, V_ANORM + l * 8)
        for tt_ in range(NT):
            ht = self.httok[tt_ % 2]
            rr([self.g_mlstm_head(tt_, h, ht) for h in range(4)], 4, stagger=4)
            rr([self.g_httrans(tt_, jj, ht) for jj in (0, 4)], 2)
        self.dbg("xn", self.xn[:, :, :], self.xn.rs, [128, 8, TB], BF16)
        self.dbg("qT", self.qT[:, :, :], self.qT.rs, [128, 8, TB], BF16)
        self.dbg("kT", self.kT[:, :, :], self.kT.rs, [128, 8, TB], BF16)
        self.dbg("ktok", self.ktok[:, :, :], self.ktok.rs, [128, NT, 1024], BF16)
        self.dbg("gt", self.gt[:, :, :], [self.gt.r], [128, NT, 16])
        self.dbg("vaug", self.vaug[:, :, :, :], self.vaug.rs, [128, NT, 4, 258], BF16)
        self.dbg("gate", self.gate[:, :, :], self.gate.rs, [128, 8, TB], BF16)
        self.dbg("ymix", self.ymix[:, :, :], self.ymix.rs, [128, 8, TB], BF16)
        self.dbg("httok", self.httok[(NT - 1) % 2][:, :], self.httok[(NT - 1) % 2].rs, [128, 1024], BF16)
        self.dbg("Cst", self.C[:, :, :, :], self.C.rs, [128, 2, 4, 257])
        self.wout("a_w_out", l, b)
        if b + 1 < self.nb and not hoist:
            self.xnorm(b + 1, V_ANORM + l * 8)


@_P(Prog)
def allocB(self, es):
    c, T = self.c, self.T
    ntile = T // 128
    self.alloc_pools(es, nw=4, nfp=16, nhp=10, nsp=12)
    self.K0T = c.sb("K0T", [128, 8, T], BF16, nres=self.nb * 8, es=es)
    self.RT = c.sb("RT", [128, T], BF16, nres=self.nb, es=es)
    self.Vtok = c.sb("Vtok", [128, ntile, 1024], BF16, nres=self.nb * 4, es=es)
    self.rstdk = c.sb("rstdk", [128, ntile, 8], F32, nres=self.nb, es=es)
    self.qn = c.sb("qn", [128, 8, TB], BF16, nres=8, es=es)
    self.qr = c.sb("qr", [128, 8, TB], BF16, nres=8, es=es)
    self.szt = c.sb("szt", [128, 8, TB], BF16, nres=8, es=es)
    self.memn = self.szt
    self.cosT = c.sb("cosT", [64, TB], F32, es=es)
    self.sinT = c.sb("sinT", [64, TB], F32, es=es)
    self.posb = c.sb("posb", [64, TB], I32, es=es)
    self.posb.chan = c.chan()
    self.cn = c.sb("cn", [128, 2, TB], BF16, es=es)
    self.sqpe = c.sb("sqpe", [64, TB], BF16, es=es)
    self.qln = c.sb("qln", [128, 3, TB], BF16, es=es)
    self.invs = c.sb("invs", [64, 1], F32, es=es)


@_P(Prog)
def ropetab(self, s, b):
    c, d = self.c, self.d
    blk = slice(b * TB, (b + 1) * TB)
    c.dma(c.sp, self.posb[:, :], d["pos"][s, :, blk], R=[self.dres["pos"]], W=[self.posb.r], chan=self.posb.chan)
    pf = self.fpool()
    c.cp(pf[0:64, 0:TB], self.posb[:, :], R=[self.posb.r], W=[pf.r])
    u = self.fpool()
    c.ts(u[0:64, 0:TB], pf[0:64, 0:TB], self.invs[0:64, 0:1], None, ALU.mult, None, R=[pf.r, self.invs.r], W=[u.r])
    pi_ = self.fpool()
    piv = pi_.t[0:64, 0:TB].bitcast(I32)
    c.cp(piv, u[0:64, 0:TB], R=[u.r], W=[pi_.r])
    nf = self.fpool()
    c.cp(nf[0:64, 0:TB], piv, R=[pi_.r], W=[nf.r])
    f = self.fpool()
    c.tt(f[0:64, 0:TB], u[0:64, 0:TB], nf[0:64, 0:TB], ALU.subtract, R=[u.r, nf.r], W=[f.r])
    s1 = self.fpool()
    c.actf(s1[0:64, 0:TB], f[0:64, 0:TB], AF.Sin, R=[f.r], W=[s1.r], scale=math.pi)
    s2 = self.fpool()
    c.actf(s2[0:64, 0:TB], f[0:64, 0:TB], AF.Sin, R=[f.r], W=[s2.r], scale=0.5 * math.pi)
    c1 = self.fpool()
    c.tt(c1[0:64, 0:TB], s2[0:64, 0:TB], s2[0:64, 0:TB], ALU.mult, R=[s2.r], W=[c1.r])
    c.ts(c1[0:64, 0:TB], c1[0:64, 0:TB], -2.0, 1.0, ALU.mult, ALU.add, R=[], W=[c1.r])
    c.stt(self.sinT[:, :], s1[0:64, 0:TB], 2.0, c1[0:64, 0:TB], ALU.mult, ALU.mult, R=[s1.r, c1.r], W=[self.sinT.r])
    c.tt(c1[0:64, 0:TB], s1[0:64, 0:TB], s1[0:64, 0:TB], ALU.mult, R=[s1.r], W=[c1.r])
    c.ts(self.cosT[:, :], c1[0:64, 0:TB], -2.0, 1.0, ALU.mult, ALU.add, R=[c1.r], W=[self.cosT.r])


@_P(Prog)
def g_rope(self, src, gcol, dst_ap, dst_res, rstd=None):
    c = self.c
    g = self.fpool()
    if rstd is None:
        c.ts(g[0:64, 0:TB], src[0:64, 0:TB], self.vecs[0:64, gcol:gcol + 1], None, ALU.mult, None,
             R=[src.r, self.vecs.r], W=[g.r])
    else:
        c.stt(g[0:64, 0:TB], src[0:64, 0:TB], self.vecs[0:64, gcol:gcol + 1], rstd[0:64, 0:TB], ALU.mult, ALU.mult,
              R=[src.r, self.vecs.r, rstd.r], W=[g.r])
    yield
    ghi = self.hpool()
    c.actf(ghi[0:64, 0:TB], g[0:64, 0:TB], AF.Copy, R=[g.r], W=[ghi.r])
    yield
    glo_f = self.fpool()
    c.tt(glo_f[0:64, 0:TB], g[0:64, 0:TB], ghi[0:64, 0:TB], ALU.subtract, R=[g.r, ghi.r], W=[glo_f.r])
    t1 = self.fpool()
    c.tt(t1[0:64, 0:TB], g[0:64, 0:TB], self.cosT[:, :], ALU.mult, R=[g.r, self.cosT.r], W=[t1.r])
    yield
    glo = self.hpool()
    c.actf(glo[0:64, 0:TB], glo_f[0:64, 0:TB], AF.Copy, R=[glo_f.r], W=[glo.r])
    yield
    pr = c.bank()
    c.mm(pr[0:64, 0:TB], self.rmT[0:64, :], ghi[0:64, 0:TB], True, False, R=[self.rmT.r, ghi.r], W=[pr.r])
    c.mm(pr[0:64, 0:TB], self.rmT[0:64, :], glo[0:64, 0:TB], False, True, R=[self.rmT.r, glo.r], W=[pr.r])
    yield
    t2 = self.fpool()
    c.tt(t2[0:64, 0:TB], pr[0:64, 0:TB], self.sinT[:, :], ALU.mult, R=[self.sinT.r], W=[t2.r, pr.r])
    yield
    c.tt(dst_ap, t1[0:64, 0:TB], t2[0:64, 0:TB], ALU.add, R=[t1.r, t2.r], W=[dst_res])


@_P(Prog)
def setupB(self):
    c, cst = self.c, self.cst
    c.memset(self.RT[64:128, :], 0.0, W=self.RT.rs)
    c.memset(self.qr[64:128, :, :], 0.0, W=self.qr.rs)
    c.ts(self.invs[:, :], cst[0:64, C_MISC:C_MISC + 1], 1.0 / (2.0 * math.pi), None, ALU.mult, None,
         R=[cst.r], W=[self.invs.r])


@_P(Prog)
def sharedkv(self, s):
    c, cst = self.c, self.cst
    for b in range(self.nb):
        blk = slice(b * TB, (b + 1) * TB)
        self.xnorm(b, V_KVNORM)
        xn = self.xn
        self.ropetab(s, b)
        wb, wv = self.loadw("w_kva", None, KVA_OFFS[0], 8, 256)
        cf = []
        pss = c.banks[4]
        for half in range(2):
            ps = c.bank()
            for k in range(8):
                c.mm(ps[:, 0:TB], wv[:, k, half * 128:(half + 1) * 128], xn[:, k, :], k == 0, k == 7,
                     R=[wb.r, xn.rs[k]], W=[ps.r])
            f = self.fpool()
            c.actf(f[:, 0:TB], ps[:, 0:TB], AF.Copy, R=[], W=[f.r, ps.r])
            sq = self.hpool()
            c.actf(sq[:, 0:TB], f[:, 0:TB], AF.Square, R=[f.r], W=[sq.r])
            c.mm(pss[:, 0:TB], self.ones[:, :], sq[:, 0:TB], half == 0, half == 1, R=[sq.r, self.ones.r], W=[pss.r])
            cf.append(f)
        rs = self.rstd_from(pss[:, 0:TB], pss.r, 128, TB, 1.0 / 256)
        for half in range(2):
            c.stt(self.cn[:, half, :], cf[half][:, 0:TB], self.vecs[:, V_KVLATG + half:V_KVLATG + half + 1], rs[:, 0:TB],
                  ALU.mult, ALU.mult, R=[cf[half].r, rs.r, self.vecs.r], W=[self.cn.r])
        wb, wv = self.loadw("w_kva", None, KVA_OFFS[1], 8, 64)
        ps = c.bank()
        for k in range(8):
            c.mm(ps[0:64, 0:TB], wv[:, k, :], xn[:, k, :], k == 0, k == 7, R=[wb.r, xn.rs[k]], W=[ps.r])
        kpf = self.fpool()
        c.actf(kpf[0:64, 0:TB], ps[0:64, 0:TB], AF.Copy, R=[], W=[kpf.r, ps.r])
        c.actf(self.sqpe[:, :], kpf[0:64, 0:TB], AF.Square, R=[kpf.r], W=[self.sqpe.r])
        rr([self.g_rope(kpf, V_KG_ROPE, self.RT[0:64, blk], self.RT.rs[b])], 1)
        ssq = c.banks[5]
        rr([self.g_knope(b, g, ssq) for g in range(4)], 2)
        rk = self.spool()
        c.ts(rk[:, 0:NT * 8], ssq[:, 0:NT * 8], 1.0 / 192, EPS, ALU.mult, ALU.add, R=[], W=[rk.r, ssq.r])
        c.actf(rk[:, 0:NT * 8], rk[:, 0:NT * 8], AF.Ln, R=[], W=[rk.r])
        c.actf(rk[:, 0:NT * 8], rk[:, 0:NT * 8], AF.Exp, R=[self.lnsc.r], W=[rk.r], scale=-0.5, bias=self.lnsc[:, 0:1])
        c.cp(self.rstdk[:, b * NT:(b + 1) * NT, :], rk[:, 0:NT * 8].rearrange("p (t h) -> p t h", h=8), R=[rk.r],
             W=[self.rstdk.rs[b]])
        rr([self.g_vgroup(b, g) for g in range(4)], 2)


@_P(Prog)
def g_knope(self, b, g, ssq):
    c = self.c
    blk = slice(b * TB, (b + 1) * TB)
    wb, wv = self.loadw("w_kvb", None, KVB_OFFS[g], 2, 256)
    yield
    for half in range(2):
        h = 2 * g + half
        ps = c.bank()
        for kc in range(2):
            c.mm(ps[:, 0:TB], wv[:, kc, half * 128:(half + 1) * 128], self.cn[:, kc, :], kc == 0, kc == 1,
                 R=[wb.r, self.cn.r], W=[ps.r])
        yield
        c.ts(self.K0T[:, h, blk], ps[:, 0:TB], self.vecs[:, V_KG_NOPE:V_KG_NOPE + 1], None, ALU.mult, None,
             R=[self.vecs.r], W=[self.K0T.rs[b * 8 + h], ps.r])
        yield
        sq = self.hpool()
        c.actf(sq[:, 0:TB], ps[:, 0:TB], AF.Square, R=[], W=[sq.r, ps.r])
        yield
        for tt_ in range(NT):
            tsl = slice(tt_ * 128, (tt_ + 1) * 128)
            col = tt_ * 8 + h
            c.mm(ssq[:, col:col + 1], sq[:, tsl], self.ones[:, 0:1], True, False, R=[sq.r, self.ones.r], W=[ssq.r])
            c.mm(ssq[:, col:col + 1], self.sqpe[0:64, tsl], self.ones[0:64, 0:1], False, True,
                 R=[self.sqpe.r, self.ones.r], W=[ssq.r])
        yield


@_P(Prog)
def g_vgroup(self, b, g):
    c = self.c
    wb, wv = self.loadw("w_kvb", None, KVB_OFFS[4 + g], 2, 256)
    yield
    for tt_ in range(NT):
        tsl = slice(tt_ * 128, (tt_ + 1) * 128)
        ps = c.bank()
        for kc in range(2):
            c.mm(ps[:, 0:256], self.cn[:, kc, tsl], wv[:, kc, :], kc == 0, kc == 1, R=[wb.r, self.cn.r], W=[ps.r])
        yield
        c.actf(self.Vtok[:, b * NT + tt_, g * 256:(g + 1) * 256], ps[:, 0:256], AF.Copy, R=[],
               W=[self.Vtok.rs[b * 4 + g], ps.r])
        yield


@_P(Prog)
def g_qhead(self, j, h):
    c = self.c
    wb, wv = self.loadw("b_w_qup", j, QUP_OFFS[h], 3, 192)
    yield
    psn = c.bank()
    for kc in range(3):
        c.mm(psn[:, 0:TB], wv[:, kc, 0:128], self.qln[:, kc, :], kc == 0, kc == 2, R=[wb.r, self.qln.r], W=[psn.r])
    psr = c.bank()
    for kc in range(3):
        c.mm(psr[0:64, 0:TB], wv[:, kc, 128:192], self.qln[:, kc, :], kc == 0, kc == 2, R=[wb.r, self.qln.r],
             W=[psr.r])
    yield
    qnf = self.fpool()
    c.actf(qnf[:, 0:TB], psn[:, 0:TB], AF.Copy, R=[], W=[qnf.r, psn.r])
    qrf = self.fpool()
    c.actf(qrf[0:64, 0:TB], psr[0:64, 0:TB], AF.Copy, R=[], W=[qrf.r, psr.r])
    yield
    sqn = self.hpool()
    c.actf(sqn[:, 0:TB], qnf[:, 0:TB], AF.Square, R=[qnf.r], W=[sqn.r])
    sqr = self.hpool()
    c.actf(sqr[0:64, 0:TB], qrf[0:64, 0:TB], AF.Square, R=[qrf.r], W=[sqr.r])
    yield
    ps2 = c.bank()
    c.mm(ps2[:, 0:TB], self.ones[:, :], sqn[:, 0:TB], True, False, R=[sqn.r, self.ones.r], W=[ps2.r])
    c.mm(ps2[:, 0:TB], self.ones[0:64, :], sqr[0:64, 0:TB], False, True, R=[sqr.r, self.ones.r], W=[ps2.r])
    yield
    rq = self.fpool()
    c.ts(rq[:, 0:TB], ps2[:, 0:TB], 1.0 / 192, EPS, ALU.mult, ALU.add, R=[], W=[rq.r, ps2.r])
    yield
    c.actf(rq[:, 0:TB], rq[:, 0:TB], AF.Ln, R=[], W=[rq.r])
    yield
    c.actf(rq[:, 0:TB], rq[:, 0:TB], AF.Exp, R=[], W=[rq.r], scale=-0.5)
    yield
    c.stt(self.qn[:, h, :], qnf[:, 0:TB], self.vecs[:, V_QG_NOPE + j:V_QG_NOPE + j + 1], rq[:, 0:TB],
          ALU.mult, ALU.mult, R=[qnf.r, rq.r, self.vecs.r], W=[self.qn.rs[h]])
    yield from self.g_rope(qrf, V_QG_ROPE + j, self.qr[0:64, h, :], self.qr.rs[h], rstd=rq)


@_P(Prog)
def g_zgroup(self, j, g):
    c, xn = self.c, self.xn
    wb, wv = self.loadw("b_w_in", j, B_OFFS[2 + g], 8, 256)
    yield
    for half in range(2):
        jz = 2 * g + half
        ps = c.bank()
        for k in range(8):
            c.mm(ps[:, 0:TB], wv[:, k, half * 128:(half + 1) * 128], xn[:, k, :], k == 0, k == 7,
                 R=[wb.r, xn.rs[k]], W=[ps.r])
        yield
        c.actf(self.szt[:, jz, :], ps[:, 0:TB], AF.Silu, R=[], W=[self.szt.rs[jz], ps.r])
        yield


@_P(Prog)
def g_attn_head(self, b, h, par):
    c = self.c
    nkt = NT * (b + 1)
    po = c.banks[4 + 2 * par]
    pd = c.banks[5 + 2 * par]
    pend = None
    for kt in range(nkt + 1):
        cur = None
        if kt < nkt:
            i = kt - NT * b
            q0 = 0 if i <= 0 else i * 128
            N = TB - q0
            kb = kt // NT
            ksl = slice(kt * 128, (kt + 1) * 128)
            pss_ = c.bank()
            c.mm(pss_[:, 0:N], self.K0T[:, h, ksl], self.qn[:, h, q0:TB], True, False,
                 R=[self.K0T.rs[kb * 8 + h], self.qn.rs[h]], W=[pss_.r])
            c.mm(pss_[:, 0:N], self.RT[:, ksl], self.qr[:, h, q0:TB], False, True,
                 R=[self.RT.rs[kb], self.qr.rs[h]], W=[pss_.r])
            p = self.hpool()
            c.actf(p[:, 0:N], pss_[:, 0:N], AF.Exp, R=[self.rstdk.rs[kb]], W=[p.r, pss_.r],
                   scale=self.rstdk[:, kt, h:h + 1])
            if i >= 0:
                c.tt(p[:, 0:128], p[:, 0:128], self.maskB[:, :], ALU.mult, R=[self.maskB.r], W=[p.r])
            cur = (kt, p, q0, N, kb)
        if pend is not None:
            kt_, p_, q0_, N_, kb_ = pend
            c.mm(po[:, q0_:TB], self.Vtok[:, kt_, h * 128:(h + 1) * 128], p_[:, 0:N_], kt_ == 0,
                 kt_ == nkt - 1, R=[self.Vtok.rs[kb_ * 4 + h // 2], p_.r], W=[po.r])
            c.mm(pd[:, q0_:TB], self.ones[:, :], p_[:, 0:N_], kt_ == 0, kt_ == nkt - 1,
                 R=[self.ones.r, p_.r], W=[pd.r])
        pend = cur
        yield
    rden = self.fpool()
    c.recip(rden[:, 0:TB], pd[:, 0:TB], R=[], W=[rden.r, pd.r])
    yield
    y1 = self.fpool()
    c.tt(y1[:, 0:TB], po[:, 0:TB], rden[:, 0:TB], ALU.mult, R=[rden.r], W=[y1.r, po.r])
    yield
    c.tt(self.ymix[:, h, :], y1[:, 0:TB], self.szt[:, h, :], ALU.mult, R=[y1.r, self.szt.rs[h]], W=[self.ymix.rs[h]])


@_P(Prog)
def layerB(self, s, j):
    c, cst = self.c, self.cst
    L = 2 + j
    self.memnorm(s)
    self.memkv(L)
    for b in range(self.nb):
        c.nrot = 8
        self.xnorm(b, V_BNORM + 8 * j)
        self.ropetab(s, b)
        xn = self.xn
        pss = c.bank()
        qlf = []
        for ci in range(3):
            if ci == 0:
                wb, wv = self.loadw("b_w_in", j, B_OFFS[0], 8, 256)
            elif ci == 2:
                wb, wv = self.loadw("b_w_in", j, B_OFFS[1], 8, 128)
            cs_ = slice((ci % 2) * 128, (ci % 2 + 1) * 128) if ci < 2 else slice(0, 128)
            ps = c.bank()
            for k in range(8):
                c.mm(ps[:, 0:TB], wv[:, k, cs_], xn[:, k, :], k == 0, k == 7, R=[wb.r, xn.rs[k]], W=[ps.r])
            f = self.fpool()
            qlf.append(f)
            c.actf(f[:, 0:TB], ps[:, 0:TB], AF.Copy, R=[], W=[f.r, ps.r])
            sq = self.hpool()
            c.actf(sq[:, 0:TB], f[:, 0:TB], AF.Square, R=[f.r], W=[sq.r])
            c.mm(pss[:, 0:TB], self.ones[:, :], sq[:, 0:TB], ci == 0, ci == 2, R=[sq.r, self.ones.r], W=[pss.r])
        rs = self.rstd_from(pss[:, 0:TB], pss.r, 128, TB, 1.0 / 384)
        for ci in range(3):
            c.stt(self.qln[:, ci, :], qlf[ci][:, 0:TB], self.vecs[:, V_QLATG + 3 * j + ci:V_QLATG + 3 * j + ci + 1],
                  rs[:, 0:TB], ALU.mult, ALU.mult, R=[qlf[ci].r, rs.r, self.vecs.r], W=[self.qln.r])
        q = [self.g_qhead(j, h) for h in range(8)]
        z = [self.g_zgroup(j, g) for g in range(4)]
        m = [("m", self.g_memgroup("b_w_in", j, B_OFFS[6 + g2], B_OFFS[8 + g2], g2, L)) for g2 in range(2)]
        qt = [("qa" if h % 2 == 0 else "qb", q[h]) for h in range(8)]
        rr([qt[0], qt[1], ("z", z[0]), qt[2], qt[3], ("z", z[1]), qt[4], qt[5], qt[6], qt[7]], 3, stagger=7)
        rr([m[0], z[2], m[1], z[3]], 2)
        c.nrot = 4
        rr([self.g_attn_head(b, h, h % 2) for h in range(8)], 2, stagger=(NT * (b + 1) + 4) // 2)
        self.wout("b_w_out", j, b)


@_P(Prog)
def cast_next(self, p):
    i = self.phases.index(p)
    if i + 1 < len(self.phases):
        self.precast(self._wl(self.phases[i + 1]))


@_P(Prog)
def build(self):
    c = self.c
    self.decl()
    c.init_psum(8)
    self.alloc_common()
    self.load_consts()
    def wlist(p):
        if p[0] == "A":
            l = int(p[1])
            return [("mem_w_kv", l), ("a_w_in", l), ("a_w_out", l)]
        if p == "KV":
            return [("w_kva", None), ("w_kvb", None)]
        j = int(p[1])
        return [("mem_w_kv", 2 + j), ("b_w_in", j), ("b_w_qup", j), ("b_w_out", j)]

    self.precast(wlist(self.phases[0]))
    self._wl = wlist
    for s in range(self.nseq):
        aph = [p for p in self.phases if p[0] == "A"]
        bph = [p for p in self.phases if p[0] != "A"]
        if aph:
            with ExitStack() as es:
                self.allocA(es)
                self.load_seq(s)
                for p in aph:
                    self.cast_next(p)
                    self.layerA(s, int(p[1]))
                c.barrier()
        if bph:
            with ExitStack() as es:
                self.allocB(es)
                c.nrot = 4
                self.setupB()
                if not aph:
                    self.load_seq(s)
                for p in bph:
                    self.cast_next(p)
                    if p == "KV":
                        self.sharedkv(s)
                    else:
                        self.layerB(s, int(p[1]))
                c.barrier()
                c.nrot = 8
        self.store_seq(s)
    c.sp.obj.wait_ge(self.ochan.sem, self.ochan.val)
    c.barrier()
    c.es.close()
    return self.nc


def prep_shared(inp):
    f = lambda a: np.ascontiguousarray(np.asarray(a, dtype=np.float32))
    out = {}
    out["consts"] = make_consts()
    out["a_w_in"] = np.stack([pack_w(f(inp["a_w_in"][l]), A_GROUPS) for l in range(2)])
    out["a_w_out"] = np.stack([pack_w(f(inp["a_w_out"][l]), OUT_GROUPS) for l in range(2)])
    out["b_w_in"] = np.stack([pack_w(f(inp["b_w_in"][l]), B_GROUPS) for l in range(2)])
    out["b_w_qup"] = np.stack([pack_w(f(inp["b_w_q_up"][l]), QUP_GROUPS) for l in range(2)])
    out["b_w_out"] = np.stack([pack_w(f(inp["b_w_out"][l]), OUT_GROUPS) for l in range(2)])
    out["w_kva"] = pack_w(f(inp["w_kv_a"]), KVA_GROUPS)
    wkvb = f(inp["w_kv_b"]).reshape(256, 8, 2, 128)
    wkvb = np.concatenate([wkvb[:, :, 0, :].reshape(256, 1024), wkvb[:, :, 1, :].reshape(256, 1024)], axis=1)
    out["w_kvb"] = pack_w(wkvb, KVB_GROUPS)
    out["mem_w_kv"] = np.stack([pack_w(f(inp["mem_w_kv"][l]), MKV_GROUPS) for l in range(4)])
    v = np.zeros((128, NVEC), np.float32)

    def col8(x):
        return f(x).reshape(8, 128).T

    for l in range(2):
        v[:, V_ANORM + 8 * l:V_ANORM + 8 * l + 8] = col8(inp["a_norm"][l])
        v[:, V_BNORM + 8 * l:V_BNORM + 8 * l + 8] = col8(inp["b_norm"][l])
        cw = f(inp["a_conv_w"][l])
        v[:, V_CONVW + 64 * l:V_CONVW + 64 * l + 64] = cw.reshape(4, 16, 128).transpose(2, 1, 0).reshape(128, 64)
        v[:, V_CONVB + 16 * l:V_CONVB + 16 * l + 16] = f(inp["a_conv_b"][l]).reshape(16, 128).T
        v[:, V_QLATG + 3 * l:V_QLATG + 3 * l + 3] = f(inp["b_q_lat_norm"][l]).reshape(3, 128).T
        v[:, V_QG_NOPE + l] = f(inp["b_q_gain"][l])[0:128]
        v[0:64, V_QG_ROPE + l] = f(inp["b_q_gain"][l])[128:192]
        v[0:4, V_IB + l] = f(inp["a_ig_bias"][l])
        v[0:4, V_NFB + l] = f(inp["a_fg_bias"][l])
    v[:, V_KVNORM:V_KVNORM + 8] = col8(inp["kv_norm"])
    v[:, V_MEMNORM:V_MEMNORM + 8] = col8(inp["mem_norm"])
    for L in range(4):
        v[:, V_MQG + L] = f(inp["mem_q_gain"][L])
        v[:, V_MKG + L] = f(inp["mem_k_gain"][L])
    v[:, V_KVLATG:V_KVLATG + 2] = f(inp["kv_lat_norm"]).reshape(2, 128).T
    v[:, V_KG_NOPE] = f(inp["k_gain"])[0:128]
    v[0:64, V_KG_ROPE] = f(inp["k_gain"])[128:192]
    out["vecs"] = v
    out["rows"] = f(inp["a_h_norm"]).reshape(1, 2048)
    return out


def prep_core(inp, b0, nseq):
    x = np.asarray(inp["x"][b0:b0 + nseq], dtype=np.float32)
    mem = np.asarray(inp["mem"][b0:b0 + nseq], dtype=np.float32)
    pos = np.asarray(inp["positions"][b0:b0 + nseq]).astype(np.int32)
    return {
        "xT": np.ascontiguousarray(x.transpose(0, 2, 1)),
        "memT": np.ascontiguousarray(mem.transpose(0, 2, 1)),
        "pos": np.ascontiguousarray(np.broadcast_to(pos[:, None, :], (nseq, 64, pos.shape[1]))),
    }


_CACHE = {}


def run(inp, phases, ncores, nseq, T, debug=False):
    key = (tuple(phases), nseq, T, debug)
    if key not in _CACHE:
        p = Prog(T, nseq, phases)
        p.debug = debug
        _CACHE[key] = p.build()
    nc = _CACHE[key]
    shared = prep_shared(inp)
    in_maps = []
    for ci in range(ncores):
        m = dict(shared)
        m.update(prep_core(inp, ci * nseq, nseq))
        in_maps.append(m)
    res = run_bass_kernel_spmd(nc, in_maps, core_ids=list(range(ncores)))
    if debug:
        return res.results
    outs = [np.asarray(r["outT"]).transpose(0, 2, 1) for r in res.results]
    return np.ascontiguousarray(np.concatenate(outs, axis=0)).astype(np.float32)


def kernel(**inputs):
    return run(inputs, ["A0", "A1", "KV", "B0", "B1"], 8, 2, 2048)
```
